# Optimizing a Trainium2 kernel written in Bass

```python
import math
import jax, jax.numpy as jnp
from jax import lax
import numpy as np

D_MODEL = 1024
BATCH = 8
SEQ = 2048
DEPTH = 4

HEAD_DIM = 64
MIX_WIDTH = D_MODEL
A_HEADS = MIX_WIDTH // 2 // HEAD_DIM
A_PATTERNS = ((128, 1), (512, 4), (2048, 16))
B_HEADS = MIX_WIDTH // 2 // (2 * HEAD_DIM)
C_HEADS = MIX_WIDTH // HEAD_DIM
C_KV_HEADS = C_HEADS // 4
C_HALF_WINDOW = 128
Q_BLOCK = 128
A_WIDTH = A_HEADS * HEAD_DIM
B_QK_WIDTH = B_HEADS * 2 * HEAD_DIM
B_V_WIDTH = B_HEADS * 2 * HEAD_DIM
EVEN_IN = 3 * A_WIDTH + 2 * B_QK_WIDTH + B_V_WIDTH + MIX_WIDTH
C_KV_WIDTH = C_KV_HEADS * HEAD_DIM
ODD_IN = MIX_WIDTH + 2 * C_KV_WIDTH + MIX_WIDTH
EPS = 1e-6
NEG_INF = -1e30

kernel_name = "hybrid_dilated_diff_swa_encoder"


def rms_norm(x, g):
    x32 = x.astype(jnp.float32)
    y = x32 * lax.rsqrt(jnp.mean(x32 * x32, axis=-1, keepdims=True) + EPS)
    return (y * g.astype(jnp.float32)).astype(x.dtype)


def alibi_slopes(n):
    return 2.0 ** (-8.0 * jnp.arange(1, n + 1, dtype=jnp.float32) / n)


def banded_attention(q, k, v, slopes, spacing, W):
    B, G, R, L, dh = q.shape
    nb = -(-L // W)
    Lp = nb * W
    pad = Lp - L
    qb = jnp.pad(q, ((0, 0), (0, 0), (0, 0), (0, pad), (0, 0))).reshape(B, G, R, nb, W, dh)

    def windows(t):
        tp = jnp.pad(t, ((0, 0), (0, 0), (W, W + pad), (0, 0))).reshape(B, G, nb + 2, W, t.shape[-1])
        return jnp.concatenate([tp[:, :, 0:nb], tp[:, :, 1:nb + 1], tp[:, :, 2:nb + 2]], axis=3)

    kw, vw = windows(k), windows(v)
    s = jnp.einsum('bgrnid,bgncd->bgrnic', qb, kw, preferred_element_type=jnp.float32)
    blk = jnp.arange(nb)[:, None, None]
    qpos = blk * W + jnp.arange(W)[None, :, None]
    kpos = blk * W - W + jnp.arange(3 * W)[None, None, :]
    rel = jnp.abs(kpos - qpos)
    valid = (rel <= W) & (kpos >= 0) & (kpos < L)
    dist = (rel * spacing).astype(jnp.float32)
    s = s - slopes.astype(jnp.float32)[None, :, :, None, None, None] * dist
    s = jnp.where(valid, s, NEG_INF)
    m = jnp.max(s, axis=-1)
    p = jnp.exp(s - m[..., None])
    l = jnp.sum(p, axis=-1)
    acc = jnp.einsum('bgrnic,bgncd->bgrnid', p, vw.astype(jnp.float32))
    m = m.reshape(B, G, R, Lp)[..., :L]
    l = l.reshape(B, G, R, Lp)[..., :L]
    acc = acc.reshape(B, G, R, Lp, -1)[:, :, :, :L]
    return m, l, acc


def dilated_mixture_attention(q, k, v):
    B, H, S, dh = q.shape
    q = q * (dh ** -0.5)
    slopes = alibi_slopes(H)
    ms, ls, accs = [], [], []
    for window, dil in A_PATTERNS:
        L = S // dil
        half = window // (2 * dil)

        def to_res(t):
            return t.reshape(B, H, L, dil, t.shape[-1]).transpose(0, 1, 3, 2, 4).reshape(B, H * dil, L, t.shape[-1])

        def from_res(t):
            t = t.reshape((B, H, dil, L) + t.shape[3:])
            t = jnp.swapaxes(t, 2, 3)
            return t.reshape((B, H, S) + t.shape[4:])

        m, l, acc = banded_attention(to_res(q)[:, :, None], to_res(k), to_res(v),
                                     jnp.repeat(slopes, dil)[:, None], dil, half)
        ms.append(from_res(m[:, :, 0]))
        ls.append(from_res(l[:, :, 0]))
        accs.append(from_res(acc[:, :, 0]))
    m = jnp.stack(ms)
    w = jnp.exp(m - jnp.max(m, axis=0))
    num = jnp.sum(w[..., None] * jnp.stack(accs), axis=0)
    den = jnp.sum(w * jnp.stack(ls), axis=0)
    return num / den[..., None]


def differential_attention(q1, q2, k1, k2, v, lam, lam_init, subln_g):
    B, H, S, dh = q1.shape
    scale = dh ** -0.5
    slopes = alibi_slopes(H)
    kpos = jnp.arange(S)
    v32 = v.astype(jnp.float32)

    def block(n):
        start = n * Q_BLOCK
        qpos = start + jnp.arange(Q_BLOCK)
        bias = -slopes[:, None, None] * jnp.abs(qpos[:, None] - kpos[None, :]).astype(jnp.float32)

        def attn(q, k):
            qs = lax.dynamic_slice_in_dim(q, start, Q_BLOCK, axis=2)
            s = jnp.einsum('bhqd,bhkd->bhqk', qs, k, preferred_element_type=jnp.float32) * scale + bias
            return jax.nn.softmax(s, axis=-1)

        a = attn(q1, k1) - lam * attn(q2, k2)
        return jnp.einsum('bhqk,bhkd->bhqd', a, v32)

    out = lax.map(block, jnp.arange(S // Q_BLOCK))
    out = jnp.moveaxis(out, 0, 2).reshape(B, H, S, 2 * dh)
    return rms_norm(out, subln_g) * (1.0 - lam_init)


def even_mixer(h, w_in, w_out, lq1, lk1, lq2, lk2, subln_g, lam_init):
    B, S, _ = h.shape
    proj = h @ w_in
    cuts = np.cumsum([A_WIDTH] * 3 + [B_QK_WIDTH] * 2 + [B_V_WIDTH]).tolist()
    qa, ka, va, qb, kb, vb, g = jnp.split(proj, cuts, axis=-1)

    def heads(t, n):
        return t.reshape(B, S, n, -1).transpose(0, 2, 1, 3)

    ya = dilated_mixture_attention(heads(qa, A_HEADS), heads(ka, A_HEADS), heads(va, A_HEADS))
    qb, kb, vb = heads(qb, B_HEADS), heads(kb, B_HEADS), heads(vb, B_HEADS)
    f32 = jnp.float32
    lam = (jnp.exp(jnp.sum(lq1.astype(f32) * lk1.astype(f32)))
           - jnp.exp(jnp.sum(lq2.astype(f32) * lk2.astype(f32))) + lam_init)
    yb = differential_attention(qb[..., :HEAD_DIM], qb[..., HEAD_DIM:],
                                kb[..., :HEAD_DIM], kb[..., HEAD_DIM:], vb, lam, lam_init, subln_g)
    y = jnp.concatenate([ya.transpose(0, 2, 1, 3).reshape(B, S, A_WIDTH),
                         yb.transpose(0, 2, 1, 3).reshape(B, S, B_V_WIDTH)], axis=-1).astype(h.dtype)
    return (y * jax.nn.silu(g)) @ w_out


def odd_mixer(h, w_in, w_out, sink):
    B, S, _ = h.shape
    R = C_HEADS // C_KV_HEADS
    proj = h @ w_in
    q, k, v, g = jnp.split(proj, [MIX_WIDTH, MIX_WIDTH + C_KV_WIDTH, MIX_WIDTH + 2 * C_KV_WIDTH], axis=-1)
    q = q.reshape(B, S, C_KV_HEADS, R, HEAD_DIM).transpose(0, 2, 3, 1, 4) * (HEAD_DIM ** -0.5)
    k = k.reshape(B, S, C_KV_HEADS, HEAD_DIM).transpose(0, 2, 1, 3)
    v = v.reshape(B, S, C_KV_HEADS, HEAD_DIM).transpose(0, 2, 1, 3)
    slopes = alibi_slopes(C_HEADS).reshape(C_KV_HEADS, R)
    m, l, acc = banded_attention(q, k, v, slopes, 1, C_HALF_WINDOW)
    sk = sink.astype(jnp.float32).reshape(C_KV_HEADS, R)[None, :, :, None]
    M = jnp.maximum(m, sk)
    e = jnp.exp(m - M)
    y = acc * e[..., None] / (l * e + jnp.exp(sk - M))[..., None]
    y = y.transpose(0, 3, 1, 2, 4).reshape(B, S, MIX_WIDTH).astype(h.dtype)
    return (y * jax.nn.silu(g)) @ w_out


def setup_inputs(seed: int = 0) -> dict:
    key = jax.random.key(seed)
    ks = jax.random.split(key, 16)
    n_even = (DEPTH + 1) // 2
    n_odd = DEPTH // 2

    def nrm(k, shape, scale):
        return jax.random.normal(k, shape, jnp.float32) * scale

    return {
        "x": nrm(ks[0], (BATCH, SEQ, D_MODEL), 1.0),
        "c": nrm(ks[1], (BATCH, D_MODEL), 1.0),
        "ada_w": nrm(ks[2], (DEPTH, D_MODEL, 3 * D_MODEL), 0.5 * D_MODEL ** -0.5),
        "ada_b": nrm(ks[3], (DEPTH, 3 * D_MODEL), 0.02),
        "norm_g": 1.0 + nrm(ks[4], (DEPTH, D_MODEL), 0.05),
        "ab_w_in": nrm(ks[5], (n_even, D_MODEL, EVEN_IN), D_MODEL ** -0.5),
        "ab_w_out": nrm(ks[6], (n_even, MIX_WIDTH, D_MODEL), MIX_WIDTH ** -0.5),
        "diff_lq1": nrm(ks[7], (n_even, HEAD_DIM), 0.1),
        "diff_lk1": nrm(ks[8], (n_even, HEAD_DIM), 0.1),
        "diff_lq2": nrm(ks[9], (n_even, HEAD_DIM), 0.1),
        "diff_lk2": nrm(ks[10], (n_even, HEAD_DIM), 0.1),
        "diff_subln_g": 1.0 + nrm(ks[11], (n_even, 2 * HEAD_DIM), 0.05),
        "c_w_in": nrm(ks[12], (n_odd, D_MODEL, ODD_IN), D_MODEL ** -0.5),
        "c_w_out": nrm(ks[13], (n_odd, MIX_WIDTH, D_MODEL), MIX_WIDTH ** -0.5),
        "c_sink": nrm(ks[14], (n_odd, C_HEADS), 1.0),
        "final_g": 1.0 + nrm(ks[15], (D_MODEL,), 0.05),
    }


def reference(x, c, ada_w, ada_b, norm_g, ab_w_in, ab_w_out, diff_lq1, diff_lk1,
              diff_lq2, diff_lk2, diff_subln_g, c_w_in, c_w_out, c_sink, final_g):
    cs = jax.nn.silu(c)
    for layer in range(DEPTH):
        mod = cs @ ada_w[layer] + ada_b[layer]
        shift, scale, gate = jnp.split(mod, 3, axis=-1)
        h = rms_norm(x, norm_g[layer]) * (1.0 + scale[:, None, :]) + shift[:, None, :]
        j = layer // 2
        if layer % 2 == 0:
            lam_init = 0.8 - 0.6 * math.exp(-0.3 * layer)
            y = even_mixer(h, ab_w_in[j], ab_w_out[j], diff_lq1[j], diff_lk1[j],
                           diff_lq2[j], diff_lk2[j], diff_subln_g[j], lam_init)
        else:
            y = odd_mixer(h, c_w_in[j], c_w_out[j], c_sink[j])
        x = x + gate[:, None, :] * y
    return rms_norm(x, final_g)
```

```python
import math
from contextlib import ExitStack

import numpy as np
import ml_dtypes
import concourse.bass as bass
import concourse.mybir as mybir
from concourse.bass_utils import run_bass_kernel_spmd

F32 = mybir.dt.float32
BF16 = mybir.dt.bfloat16
AF = mybir.ActivationFunctionType
ALU = mybir.AluOpType
AX = mybir.AxisListType

S = 2048
D = 1024
NT = 16
KC = 8
DEPTH = 4
EPS = 1e-6
U_DIL, C_DIL = 2432, 1152
U_DIF, C_DIF = 3968, 1920
U_ODD, C_ODD = 640, 256
_NLAYERS = DEPTH
_NFILL = 0
_LOOK = 4


class Sched:
    def __init__(self, nc, es):
        self.nc = nc
        self.es = es
        self.eng = {"pe": nc.tensor, "act": nc.scalar, "dve": nc.vector,
                    "pool": nc.gpsimd, "sp": nc.sync}
        self.semh = {}
        self.cnt = {}
        for e in self.eng:
            self.semh[e] = es.enter_context(nc.semaphore("s_" + e))
            self.cnt[e] = 0
        self.seen = {}
        self.last_w = {}
        self.readers = {}
        self.trace = {e: [] for e in self.eng}

    def stream(self, name):
        if name not in self.semh:
            self.semh[name] = self.es.enter_context(self.nc.semaphore("d_" + name))
            self.cnt[name] = 0

    def _wait(self, eng, tok):
        if tok is None:
            return
        sk, v = tok
        if self.seen.get((eng, sk), 0) >= v:
            return
        self.eng[eng].wait_ge(self.semh[sk], v)
        self.trace[eng].append(("w", sk, v))
        self.seen[(eng, sk)] = v

    def _deps(self, eng, reads, writes):
        for k in reads:
            t = self.last_w.get(k)
            if t is not None and not (eng == "pe" and t[0] == "pe"):
                self._wait(eng, t)
        for k in writes:
            t = self.last_w.get(k)
            if t is not None and not (eng == "pe" and t[0] == "pe"):
                self._wait(eng, t)
            for sk, v in self.readers.get(k, {}).items():
                if not (eng == "pe" and sk == "pe"):
                    self._wait(eng, (sk, v))

    def _commit(self, tok, reads, writes):
        sk, v = tok
        for k in reads:
            r = self.readers.setdefault(k, {})
            if r.get(sk, 0) < v:
                r[sk] = v
        for k in writes:
            self.last_w[k] = tok
            self.readers[k] = {}

    def op(self, eng, fn, reads=(), writes=(), inc=True):
        self._deps(eng, reads, writes)
        inst = fn()
        if inc:
            self.cnt[eng] += 1
            inst.then_inc(self.semh[eng], 1)
            self.trace[eng].append(("i", eng, 1))
            tok = (eng, self.cnt[eng])
        else:
            tok = (eng, self.cnt[eng] + 1)
        self._commit(tok, reads, writes)
        return tok

    def dma(self, q, stream, items):
        self.stream(stream)
        for (o, i, reads, writes) in items:
            self._deps(q, reads, writes)
        for (o, i, reads, writes) in items:
            inst = self.eng[q].dma_start(out=o, in_=i)
            self.cnt[stream] += 16
            inst.then_inc(self.semh[stream], 16)
            self.trace[q].append(("i", stream, 16))
        tok = (stream, self.cnt[stream])
        for (o, i, reads, writes) in items:
            self._commit(tok, reads, writes)
        return tok

    def barrier(self):
        for e in self.eng:
            for sk in list(self.semh.keys()):
                if sk != e and self.cnt[sk] > 0:
                    self._wait(e, (sk, self.cnt[sk]))


def _slopes(n):
    return 2.0 ** (-8.0 * np.arange(1, n + 1, dtype=np.float64) / n)


def _toeplitz(U, C, fn):
    i = np.arange(128, dtype=np.int64)[:, None]
    u = np.arange(U, dtype=np.int64)[None, :]
    d = np.abs(u - i - C)
    return fn(d).astype(np.float32).astype(ml_dtypes.bfloat16)


def _tables():
    sd = _slopes(8)
    dil = np.stack([
        _toeplitz(U_DIL, C_DIL, lambda d, s=s: (
            (d <= 64).astype(np.float64)
            + ((d % 4 == 0) & (d <= 256)).astype(np.float64)
            + ((d % 16 == 0) & (d <= 1024)).astype(np.float64)) * np.exp(-s * d))
        for s in sd])
    sf = _slopes(4)
    dif = np.stack([_toeplitz(U_DIF, C_DIF, lambda d, s=s: np.exp(-s * d)) for s in sf])
    so = _slopes(16)
    odd = np.stack([
        _toeplitz(U_ODD, C_ODD, lambda d, s=s: (d <= 128).astype(np.float64) * np.exp(-s * d))
        for s in so])
    return dil, dif, odd


def build_nc():
    if "tabs" not in _CACHE:
        _CACHE["tabs"] = _tables()
    np_dil, np_dif, np_odd = [np.asarray(t).astype(np.float32) != 0 for t in _CACHE["tabs"]]
    nc = bass.Bass("TRN2", target_bir_lowering=False)
    dram = lambda n, sh, dt, kind="ExternalInput": nc.dram_tensor(n, sh, dt, kind=kind)
    x_d = dram("x", [S, D], F32).ap()
    ct_d = dram("ct", [128, KC], F32).ap()
    adaw_d = dram("ada_w", [DEPTH, D, 3 * D], F32).ap()
    adab_d = dram("adab", [128, DEPTH, 24], F32).ap()
    ng_d = dram("ng", [128, DEPTH, 8], F32).ap()
    abin_d = dram("ab_w_in", [2, D, 4096], F32).ap()
    about_d = dram("ab_w_out", [2, D, D], F32).ap()
    lq1_d = dram("lq1", [2, 64], F32).ap()
    lk1_d = dram("lk1", [2, 64], F32).ap()
    lq2_d = dram("lq2", [2, 64], F32).ap()
    lk2_d = dram("lk2", [2, 64], F32).ap()
    subg_d = dram("subg", [2, 128], F32).ap()
    cin_d = dram("c_w_in", [2, D, 2560], F32).ap()
    cout_d = dram("c_w_out", [2, D, D], F32).ap()
    sink_d = dram("c_sink", [2, 16], F32).ap()
    fg_d = dram("final_g", [1, D], F32).ap()
    tdil_d = dram("t_dil", [8, 128, U_DIL], BF16).ap()
    tdif_d = dram("t_dif", [4, 128, U_DIF], BF16).ap()
    todd_d = dram("t_odd", [16, 128, U_ODD], BF16).ap()
    idb_d = dram("idb", [128, 128], BF16).ap()
    idf_d = dram("idf", [128, 128], F32).ap()
    out_d = dram("out", [S, D], F32, kind="ExternalOutput").ap()

    es = ExitStack()
    with es:
        sb = lambda n, sh, dt: es.enter_context(nc.sbuf_tensor(n, sh, dt))
        X = sb("X", [128, NT, D], F32)
        HT = sb("HT", [128, KC, S], BF16)
        YG = sb("YG", [128, NT, D], BF16)
        WB = sb("WB", [128, 2, KC, 256], BF16)
        QT = sb("QT", [128, 2, S], BF16)
        KZ = sb("KZ", [128, 2, 2, S], BF16)
        WP = sb("WP", [128, KC, 384], BF16)
        VA = sb("VA", [128, 2, NT * 130], BF16)
        E = sb("E", [128, 3, 512], BF16)
        TOE = sb("TOE", [128, 2, U_DIF], BF16)
        GREP = sb("GREP", [128, D], F32)
        IDB = sb("IDB", [128, 128], BF16)
        IDF = sb("IDF", [128, 128], F32)
        MOD = sb("MOD", [128, DEPTH, 24], F32)
        CS = sb("CS", [128, KC], F32)
        SS = sb("SS", [128, NT], F32)
        RSTD = sb("RSTD", [128, NT], F32)
        SUBG = sb("SUBG", [128, 2, 128], F32)
        LSUM = sb("LSUM", [128, 4], F32)
        NEGLAM = sb("NEGLAM", [128, 2], F32)
        ESINK = sb("ESINK", [128, 2, 16], F32)
        RC = sb("RC", [128, 8], F32)
        T1 = sb("T1", [128, 2, 128], F32)
        T2 = sb("T2", [128, 2, 128], F32)
        OC = sb("OC", [128, 2, 2, 132], F32)
        SQ = sb("SQ", [128, 8], F32)
        TMPO = sb("TMPO", [128, 2, 2, 256], F32)
        EPSC = sb("EPSC", [128, 1], F32)

        pst = lambda n: es.enter_context(nc.psum_tensor(n, [128, 512], F32))
        ST4 = es.enter_context(nc.psum_tensor("ST4", [128, 2048], F32))
        OA = [pst("OA%d" % i) for i in range(2)]
        MISCS = [pst("MISC%d" % i) for i in range(2)]
        SCR = MISCS + OA
        SCRK = [("MISC", 0), ("MISC", 1), ("OA", 0), ("OA", 1)]
        misc_i = [0]

        def misc():
            i = misc_i[0] % 2
            misc_i[0] += 1
            return MISCS[i], ("MISC", i)

        sc = Sched(nc, es)
        WBf = WB[:, :, :, :].rearrange("p a b c -> p (a b c)").bitcast(F32)
        LQK = WBf[:, 0:512].rearrange("p (a b c) -> p a b c", a=4, b=2)
        LPR = WBf[:, 512:768].rearrange("p (a b c) -> p a b c", a=2, b=2)
        ADAB = WBf[:, 768:864].rearrange("p (a b) -> p a b", a=DEPTH)
        NG = WBf[:, 864:896].rearrange("p (a b) -> p a b", a=DEPTH)
        scr_i = [0]

        def scratch():
            i = scr_i[0] % 4
            scr_i[0] += 1
            return SCR[i], SCRK[i]

        HTflat = HT[:, :, :].rearrange("p a b -> p (a b)")
        AW = HTflat[:, 0:12288].bitcast(F32)
        OUTB = HTflat[:, 0:4096].bitcast(F32)
        HTK = lambda tg: tuple(("HT", tg, fc) for fc in range(KC))

        xv = x_d.rearrange("(t p) d -> p t d", p=128)
        sc.dma("sp", "xload", [(X[:, t, :], xv[:, t, :], (), (("X", t),)) for t in range(NT)])
        small = [
            (CS[:, :], ct_d, (), (("CS",),)),
            (ADAB[:, :, :], adab_d, (), (("ADAB",),)),
            (NG[:, :, :], ng_d, (), (("NG",),)),
            (IDB[:, :], idb_d, (), (("IDB",),)),
            (IDF[:, :], idf_d, (), (("IDF",),)),
        ]
        for j in range(2):
            small.append((SUBG[:, j, :], subg_d[j].partition_broadcast(128), (), (("SUBG",),)))
            for qi, dd in enumerate((lq1_d, lk1_d, lq2_d, lk2_d)):
                small.append((LQK[:, qi, j, :], dd[j].partition_broadcast(128), (), (("LQK",),)))
            small.append((ESINK[:, j, :], sink_d[j].partition_broadcast(128), (), (("ESINK",),)))
        sc.dma("sp", "small", small)

        sc.op("act", lambda: nc.scalar.activation(out=CS[:, :], in_=CS[:, :], func=AF.Silu),
              reads=(("CS",),), writes=(("CS",),))
        sc.op("dve", lambda: nc.vector.memset(MOD[:, :, :], 0.0), writes=(("MOD",),))
        sc.op("dve", lambda: nc.vector.memset(EPSC[:, :], EPS), writes=(("EPSC",),))

        YGflat = YG[:, :, :].rearrange("p a b -> p (a b)")
        ACC3 = YGflat[:, 0:6144].bitcast(F32)
        ONESF = YGflat[:, 6144:6146].bitcast(F32)
        QKflat = KZ[:, :, :, :].rearrange("p a b c -> p (a b c)")
        AWS = [AW[:, 0:3072], AW[:, 3072:6144], YGflat[:, 8192:8192 + 6144].bitcast(F32),
               QKflat[:, 0:6144].bitcast(F32), TOE[:, :, :].rearrange("p a b -> p (a b)")[:, 0:6144].bitcast(F32)]
        sc.op("dve", lambda: nc.vector.memset(ONESF, 1.0), writes=(("ONESF",),))
        CSR = VA[:, :, :].rearrange("p a b -> p (a b)")[:, 0:2048].bitcast(F32).rearrange(
            "p (a b) -> p a b", a=KC)
        sc.op("dve", lambda: nc.vector.tensor_copy(
            out=CSR, in_=CS[:, :].unsqueeze(2).broadcast_to([128, KC, 128])),
            reads=(("CS",),), writes=(("CSR",),))
        GP = ST4[:, 0:1024]
        TP = ST4[:, 1024:2048]
        awi = 0
        for l in range(DEPTH):
            for kc in range(KC):
                slot = awi % 5
                awq = ("sp", "act", "pool")[awi % 3]
                awi += 1
                awv = AWS[slot]
                sc.dma(awq, "aw%d" % slot,
                       [(awv, adaw_d[l, kc * 128:(kc + 1) * 128, :], (), (("AW", slot),))])
                if kc == 0:
                    sc.op("dve", lambda: nc.vector.tensor_scalar(
                        out=ACC3[:, 0:2048], in0=awv[:, 0:2048], scalar1=CS[:, kc:kc + 1], scalar2=None,
                        op0=ALU.mult),
                        reads=(("AW", slot), ("CS",)), writes=(("ACC3",),))
                else:
                    sc.op("dve", lambda: nc.vector.scalar_tensor_tensor(
                        out=ACC3[:, 0:2048], in0=awv[:, 0:2048], scalar=CS[:, kc:kc + 1], in1=ACC3[:, 0:2048],
                        op0=ALU.mult, op1=ALU.add),
                        reads=(("AW", slot), ("CS",), ("ACC3",)), writes=(("ACC3",),))
                for cg in range(2):
                    sc.op("pe", lambda cg=cg: nc.tensor.matmul(
                        GP[:, cg * 512:(cg + 1) * 512], lhsT=CSR[:, kc, :],
                        rhs=awv[:, 2048 + cg * 512:2048 + (cg + 1) * 512],
                        start=(kc == 0), stop=(kc == KC - 1)),
                        reads=(("AW", slot), ("CSR",)), writes=(("GP", cg),), inc=(cg == 1))
            bank, bkey = scratch()
            for jc in range(16):
                sc.op("pe", lambda jc=jc: nc.tensor.matmul(
                    bank[:, jc:jc + 1], lhsT=ACC3[:, jc * 128:(jc + 1) * 128],
                    rhs=ONESF, start=True, stop=True),
                    reads=(("ACC3",), ("ONESF",)), writes=(bkey,), inc=(jc == 15))
            sc.op("dve", lambda: nc.vector.tensor_copy(out=MOD[:, l, 0:16], in_=bank[:, 0:16]),
                  reads=(bkey,), writes=(("MOD",),))
            sc.op("act", lambda: nc.scalar.copy(out=GREP[:, :], in_=GP),
                  reads=(("GP", 0), ("GP", 1)), writes=tuple(("GREP", fc) for fc in range(KC)))
            for fc in range(KC):
                sc.op("pe", lambda fc=fc: nc.tensor.transpose(
                    TP[:, fc * 128:(fc + 1) * 128], GREP[:, fc * 128:(fc + 1) * 128], IDF[:, :]),
                    reads=(("GREP", fc), ("IDF",)), writes=(("TP",),), inc=(fc == KC - 1))
            sc.op("dve", lambda: nc.vector.tensor_copy(
                out=MOD[:, l, 16:24], in_=TP.rearrange("p (a b) -> p a b", a=KC)[:, :, 0]),
                reads=(("TP",),), writes=(("MOD",),))
        sc.op("dve", lambda: nc.vector.tensor_tensor(
            out=MOD[:, :, :], in0=MOD[:, :, :], in1=ADAB[:, :, :], op=ALU.add),
            reads=(("MOD",), ("ADAB",)), writes=(("MOD",),))
        sc.op("dve", lambda: nc.vector.scalar_tensor_tensor(
            out=MOD[:, :, 8:16], in0=MOD[:, :, 8:16], scalar=1.0, in1=NG[:, :, :],
            op0=ALU.add, op1=ALU.mult),
            reads=(("MOD",), ("NG",)), writes=(("MOD",),))

        sc.op("dve", lambda: nc.vector.tensor_tensor(
            out=LPR[:, 0, :, :], in0=LQK[:, 0, :, :], in1=LQK[:, 1, :, :], op=ALU.mult),
            reads=(("LQK",),), writes=(("LPR",),))
        sc.op("dve", lambda: nc.vector.tensor_tensor(
            out=LPR[:, 1, :, :], in0=LQK[:, 2, :, :], in1=LQK[:, 3, :, :], op=ALU.mult),
            reads=(("LQK",),), writes=(("LPR",),))
        sc.op("dve", lambda: nc.vector.tensor_reduce(
            out=LSUM[:, :], in_=LPR[:, :, :, :].rearrange("p a b c -> p (a b) c"),
            axis=AX.X, op=ALU.add),
            reads=(("LPR",),), writes=(("LSUM",),))
        sc.op("act", lambda: nc.scalar.activation(out=LSUM[:, :], in_=LSUM[:, :], func=AF.Exp),
              reads=(("LSUM",),), writes=(("LSUM",),))
        sc.op("dve", lambda: nc.vector.tensor_tensor(
            out=NEGLAM[:, :], in0=LSUM[:, 2:4], in1=LSUM[:, 0:2], op=ALU.subtract),
            reads=(("LSUM",),), writes=(("NEGLAM",),))
        lam_inits = [0.8 - 0.6 * math.exp(-0.3 * 0), 0.8 - 0.6 * math.exp(-0.3 * 2)]
        for j in range(2):
            sc.op("dve", lambda j=j: nc.vector.tensor_scalar(
                out=NEGLAM[:, j:j + 1], in0=NEGLAM[:, j:j + 1], scalar1=-lam_inits[j],
                scalar2=None, op0=ALU.add),
                reads=(("NEGLAM",),), writes=(("NEGLAM",),))
            sc.op("dve", lambda j=j: nc.vector.tensor_scalar(
                out=SUBG[:, j, :], in0=SUBG[:, j, :], scalar1=1.0 - lam_inits[j],
                scalar2=None, op0=ALU.mult),
                reads=(("SUBG",),), writes=(("SUBG",),))
        sc.op("act", lambda: nc.scalar.activation(out=ESINK[:, :, :], in_=ESINK[:, :, :], func=AF.Exp),
              reads=(("ESINK",),), writes=(("ESINK",),))
        sc.barrier()
        sc.op("pool", lambda: nc.gpsimd.memset(QKflat, 0.0),
              writes=tuple(("K", sl, tg) for sl in range(2) for tg in range(4)))

        def rms_stats(l):
            sc.op("dve", lambda: nc.vector.memset(SS[:, :], 0.0), writes=tuple(("SS", t) for t in range(NT)))
            for t in range(NT):
                junk = TMPO[:, t % 2, :, :].rearrange("p a b -> p (a b)").bitcast(BF16)
                if t % 2 == 0:
                    sc.op("act", lambda t=t, junk=junk: nc.scalar.activation(
                        out=junk, in_=X[:, t, :], func=AF.Square, accum_out=SS[:, t:t + 1]),
                        reads=(("X", t),), writes=(("TMPO", t % 2), ("SS", t)))
                else:
                    sc.op("dve", lambda t=t, junk=junk: nc.vector.scalar_tensor_tensor(
                        out=junk, in0=X[:, t, :], scalar=1.0, in1=X[:, t, :], op0=ALU.mult, op1=ALU.mult,
                        accum_out=SS[:, t:t + 1]),
                        reads=(("X", t),), writes=(("TMPO", t % 2), ("SS", t)))
            sc.op("act", lambda: nc.scalar.activation(
                out=RSTD[:, :], in_=SS[:, :], func=AF.Ln, scale=1.0 / D, bias=EPSC[:, 0:1]),
                reads=tuple(("SS", t) for t in range(NT)) + (("EPSC",),), writes=(("RSTD",),))
            sc.op("act", lambda: nc.scalar.activation(
                out=RSTD[:, :], in_=RSTD[:, :], func=AF.Exp, scale=-0.5),
                reads=(("RSTD",),), writes=(("RSTD",),))

        def transpose_to_HT(src_is_xn, l):
            for tg in range(4):
                for fc in range(KC):
                    bank, bkey = scratch()
                    bb = bank[:, :].bitcast(BF16)
                    for i in range(4):
                        t = tg * 4 + i
                        sc.op("pe", lambda i=i, t=t: nc.tensor.transpose(
                            bb[:, i * 128:(i + 1) * 128], YG[:, t, fc * 128:(fc + 1) * 128], IDB[:, :]),
                            reads=(("YG", t), ("IDB",)), writes=(bkey,), inc=(i == 3))
                    dst = HT[:, fc, tg * 512:(tg + 1) * 512]
                    if src_is_xn:
                        sc.op("dve", lambda: nc.vector.tensor_scalar(
                            out=dst, in0=bb[:, 0:512], scalar1=MOD[:, l, 8 + fc:9 + fc],
                            scalar2=MOD[:, l, fc:fc + 1], op0=ALU.mult, op1=ALU.add),
                            reads=(bkey, ("MOD",)), writes=(("HT", tg, fc),))
                    else:
                        sc.op("act", lambda: nc.scalar.copy(out=dst, in_=bb[:, 0:512]),
                              reads=(bkey,), writes=(("HT", tg, fc),))

        def layer(l):
            even = (l % 2 == 0)
            j = l // 2
            w_in = abin_d[j] if even else cin_d[j]
            w_out = about_d[j] if even else cout_d[j]
            gcol0 = 3072 if even else 1536
            w_in_v = w_in.rearrange("(kc p) n -> p kc n", p=128)
            w_out_v = w_out.rearrange("(kc p) n -> p kc n", p=128)

            rms_stats(l)
            for t in range(NT):
                sc.op("dve", lambda t=t: nc.vector.tensor_scalar(
                    out=YG[:, t, :], in0=X[:, t, :], scalar1=RSTD[:, t:t + 1], scalar2=None,
                    op0=ALU.mult),
                    reads=(("X", t), ("RSTD",)), writes=(("YG", t),))
            for fc in range(KC):
                bank, bkey = scratch()
                sc.op("pe", lambda fc=fc: nc.tensor.matmul(
                    bank[:, 0:128], lhsT=MOD[:, l, 16 + fc:17 + fc].broadcast_to([128, 128]),
                    rhs=IDF[:, :], start=True, stop=True),
                    reads=(("MOD",), ("IDF",)), writes=(bkey,))
                sc.op("act", lambda fc=fc: nc.scalar.copy(
                    out=GREP[:, fc * 128:(fc + 1) * 128], in_=bank[:, 0:128]),
                    reads=(bkey,), writes=(("GREP", fc),))
            transpose_to_HT(True, l)

            for c in range(4):
                slot = c % 2
                sc.dma("pool", "wb%d" % slot,
                       [(WB[:, slot, :, :], w_in_v[:, :, gcol0 + c * 256: gcol0 + (c + 1) * 256],
                         (), (("WB", slot),))])
                for tp in range(NT // 2):
                    bank, bkey = scratch()
                    for i in range(2):
                        t = tp * 2 + i
                        for kc in range(KC):
                            sc.op("pe", lambda i=i, t=t, kc=kc: nc.tensor.matmul(
                                bank[:, i * 256:(i + 1) * 256],
                                lhsT=HT[:, kc, t * 128:(t + 1) * 128], rhs=WB[:, slot, kc, :],
                                start=(kc == 0), stop=(kc == KC - 1)),
                                reads=HTK(t // 4) + (("WB", slot),), writes=(bkey,),
                                inc=(i == 1 and kc == KC - 1))
                    sc.op("act", lambda tp=tp: nc.scalar.activation(
                        out=YG[:, tp * 2:tp * 2 + 2, c * 256:(c + 1) * 256],
                        in_=bank[:, :].rearrange("p (a b) -> p a b", a=2), func=AF.Silu),
                        reads=(bkey,), writes=(("YG", tp * 2), ("YG", tp * 2 + 1)))

            def pair_info(p):
                if even and p < 4:
                    return dict(kind="dil", q0=128 * p, k0=512 + 128 * p, kw=128, v0=1024 + 128 * p,
                                vw=128, dv=64, mix=128 * p,
                                tabs=[(tdil_d[2 * p], U_DIL), (tdil_d[2 * p + 1], U_DIL)], C=C_DIL,
                                nz=[np_dil[2 * p], np_dil[2 * p + 1]], dmin=-1152, dmax=1024)
                if even:
                    jh = p - 4
                    return dict(kind="dif", q0=1536 + 128 * jh, k0=2048 + 128 * jh, kw=128,
                                v0=2560 + 128 * jh, vw=128, dv=128, mix=512 + 128 * jh,
                                tabs=[(tdif_d[jh], U_DIF)], C=C_DIF, nz=[np_dif[jh]],
                                dmin=-1920, dmax=1792)
                g = p // 2
                return dict(kind="odd", q0=128 * p, k0=1024 + 64 * g, kw=64, v0=1280 + 64 * g,
                            vw=64, dv=64, mix=128 * p,
                            tabs=[(todd_d[2 * p], U_ODD), (todd_d[2 * p + 1], U_ODD)], C=C_ODD,
                            nz=[np_odd[2 * p], np_odd[2 * p + 1]], dmin=-256, dmax=128)

            def va_view(slot, info):
                if info["kind"] == "dil":
                    return VA[:, slot, 0:NT * 130].rearrange("p (t h c) -> p t h c", t=NT, h=2)
                if info["kind"] == "dif":
                    return VA[:, slot, 0:NT * 129].rearrange("p (t c) -> p t c", t=NT)
                return VA[:, slot, 0:NT * 65].rearrange("p (t c) -> p t c", t=NT)

            def kv_slot(p):
                return (p // 2) % 2 if not even else p % 2

            def need_kv(p):
                return even or p % 2 == 0

            def proj(p):
                info = pair_info(p)
                slot = p % 2
                items = [(WP[:, :, 0:128], w_in_v[:, :, info["q0"]:info["q0"] + 128], (), (("WP",),))]
                if not need_kv(p):
                    pass
                elif info["kw"] == 128:
                    items.append((WP[:, :, 128:256], w_in_v[:, :, info["k0"]:info["k0"] + 128], (), (("WP",),)))
                else:
                    items.append((WP[:, :, 128:192], w_in_v[:, :, info["k0"]:info["k0"] + 64], (), (("WP",),)))
                    items.append((WP[:, :, 192:256], w_in_v[:, :, info["k0"]:info["k0"] + 64], (), (("WP",),)))
                if need_kv(p):
                    items.append((WP[:, :, 256:256 + info["vw"]],
                                  w_in_v[:, :, info["v0"]:info["v0"] + info["vw"]], (), (("WP",),)))
                sc.dma("pool", "wp", items)
                proj_rest(p, info, slot)

            def proj_tables(p):
                info = pair_info(p)
                slot = p % 2
                if info["kind"] == "dif":
                    tab, U = info["tabs"][0]
                    sc.dma("sp", "toe%d" % slot, [(TOE[:, slot, 0:U], tab, (), (("TOE", slot),))])
                elif info["kind"] == "dil":
                    for h in range(2):
                        tab, U = info["tabs"][h]
                        sc.dma("sp", "toe%d" % h, [(TOE[:, h, 0:U], tab, (), (("TOE", h),))])
                else:
                    sc.dma("sp", "toe%d" % slot,
                           [(TOE[:, slot, h * U_ODD:(h + 1) * U_ODD], info["tabs"][h][0], (), (("TOE", slot),))
                            for h in range(2)])

            def proj_rest(p, info, slot):
                ks = kv_slot(p)
                for which in range(2):
                    if which == 1 and not need_kv(p):
                        continue
                    dslot = slot if which == 0 else ks
                    for tg in range(4):
                        MISC, mkey = misc()
                        for kc in range(KC):
                            sc.op("pe", lambda kc=kc, tg=tg, which=which: nc.tensor.matmul(
                                MISC[:, :], lhsT=WP[:, kc, which * 128:(which + 1) * 128],
                                rhs=HT[:, kc, tg * 512:(tg + 1) * 512],
                                start=(kc == 0), stop=(kc == KC - 1)),
                                reads=(("WP",),) + HTK(tg), writes=(mkey,), inc=(kc == KC - 1))
                        if which == 0:
                            sc.op("dve", lambda tg=tg: nc.vector.tensor_copy(
                                out=QT[:, dslot, tg * 512:(tg + 1) * 512], in_=MISC[:, :]),
                                reads=(mkey,), writes=(("Q", dslot, tg),))
                        else:
                            for hh in range(2):
                                rws = slice(hh * 64, hh * 64 + 64)
                                sc.op("dve", lambda tg=tg, hh=hh, rws=rws: nc.vector.tensor_copy(
                                    out=KZ[rws, dslot, hh, tg * 512:(tg + 1) * 512], in_=MISC[rws, :]),
                                    reads=(mkey,), writes=(("K", dslot, tg),))
                if not need_kv(p):
                    return
                slot = ks
                vv = va_view(slot, info)
                vw = info["vw"]
                if info["kind"] == "dil":
                    sc.op("pool", lambda: nc.gpsimd.memset(vv[:, :, :, 64:65], 1.0), writes=tuple(("VA", slot, q) for q in range(4)))
                elif info["kind"] == "dif":
                    sc.op("pool", lambda: nc.gpsimd.memset(vv[:, :, 128:129], 1.0), writes=tuple(("VA", slot, q) for q in range(4)))
                else:
                    sc.op("pool", lambda: nc.gpsimd.memset(vv[:, :, 64:65], 1.0), writes=tuple(("VA", slot, q) for q in range(4)))
                for tg in range(4):
                    MISC, mkey = misc()
                    for i in range(4):
                        t = tg * 4 + i
                        for kc in range(KC):
                            sc.op("pe", lambda kc=kc, t=t, i=i: nc.tensor.matmul(
                                MISC[:, i * 128:i * 128 + vw],
                                lhsT=HT[:, kc, t * 128:(t + 1) * 128], rhs=WP[:, kc, 256:256 + vw],
                                start=(kc == 0), stop=(kc == KC - 1)),
                                reads=(("WP",),) + HTK(tg), writes=(mkey,),
                                inc=(i == 3 and kc == KC - 1))
                    src = MISC[:, :].rearrange("p (a b) -> p a b", a=4)
                    if info["kind"] == "dil":
                        sc.op("dve", lambda tg=tg: nc.vector.tensor_copy(
                            out=vv[:, tg * 4:tg * 4 + 4, :, 0:64],
                            in_=src.rearrange("p a (h c) -> p a h c", h=2)),
                            reads=(mkey,), writes=(("VA", slot, tg),))
                    elif info["kind"] == "dif":
                        sc.op("dve", lambda tg=tg: nc.vector.tensor_copy(
                            out=vv[:, tg * 4:tg * 4 + 4, 0:128], in_=src),
                            reads=(mkey,), writes=(("VA", slot, tg),))
                    else:
                        sc.op("dve", lambda tg=tg: nc.vector.tensor_copy(
                            out=vv[:, tg * 4:tg * 4 + 4, 0:64], in_=src[:, :, 0:64]),
                            reads=(mkey,), writes=(("VA", slot, tg),))

            def attend(p, mid_hook):
                info = pair_info(p)
                slot = p % 2
                kind = info["kind"]
                dv = info["dv"]
                st = 256
                ks = kv_slot(p)
                vv = va_view(ks, info)
                mix = info["mix"]
                steps = []
                per_group = 0
                for g in range(8):
                    kts = []
                    for kt in range(NT):
                        dlt = g * 256 - kt * 128
                        if not (info["dmin"] <= dlt <= info["dmax"]):
                            continue
                        o_ = dlt + info["C"]
                        if not any(z[:, o_:o_ + 256].any() for z in info["nz"]):
                            continue
                        kts.append(kt)
                    per_group = max(per_group, len(kts))
                    for i, kt in enumerate(kts):
                        steps.append((g, kt, i == 0, i == len(kts) - 1))
                n = len(steps)
                pending = []

                def oa_bank(g, half):
                    b = (g % 2) if dv == 64 else half
                    return OA[b], ("OA", b)

                def oa_off(half, jq):
                    return (half * 2 + jq) * 128 if dv == 64 else jq * 256

                def st_loc(idx):
                    return idx % 2, 0

                def emit_s(idx):
                    g, kt, first, last = steps[idx]
                    b = idx % 4
                    for half in range(2):
                        c0 = b * 512 + half * 256
                        sc.op("pe", lambda: nc.tensor.matmul(
                            ST4[:, c0:c0 + 256],
                            lhsT=KZ[:, ks, half, kt * 128:(kt + 1) * 128],
                            rhs=QT[:, slot, g * 256:(g + 1) * 256], start=True, stop=True),
                            reads=(("Q", slot, g // 2), ("K", ks, kt // 4)),
                            writes=(("ST", b),), inc=(half == 1))

                def emit_softmax(idx):
                    g, kt, first, last = steps[idx]
                    r = idx % 3
                    b = idx % 4
                    atok = sc.op("act", lambda: nc.scalar.activation(
                        out=E[:, r, :], in_=ST4[:, b * 512:(b + 1) * 512], func=AF.Exp, scale=0.125),
                        reads=(("ST", b),), writes=(("E", r),))
                    off = g * 256 - kt * 128 + info["C"]
                    ev = E[:, r, :].rearrange("p (h q) -> p h q", h=2)
                    if kind == "dif":
                        tab = TOE[:, slot, off:off + 256].unsqueeze(1).broadcast_to([128, 2, 256])
                        tkeys = (("TOE", slot),)
                    elif kind == "dil":
                        tab = TOE[:, 0:2, off:off + 256]
                        tkeys = (("TOE", 0), ("TOE", 1))
                    else:
                        tab = TOE[:, slot, 0:2 * U_ODD].rearrange("p (h u) -> p h u", h=2)[:, :, off:off + 256]
                        tkeys = (("TOE", slot),)
                    dtok = sc.op("dve", lambda: nc.vector.tensor_tensor(out=ev, in0=ev, in1=tab, op=ALU.mult),
                                 reads=(("E", r),) + tkeys, writes=(("E", r),))
                    return atok, dtok

                def emit_pv(idx, hoist_tok=None):
                    g, kt, first, last = steps[idx]
                    r = idx % 3
                    for half in range(2):
                        bank, bkey = oa_bank(g, half)
                        rhs = vv[:, kt, half, :] if kind == "dil" else vv[:, kt, :]
                        for jq in range(2):
                            if half == 1 and jq == 1 and hoist_tok is not None:
                                sc._wait("pe", hoist_tok)
                            c0 = half * 256 + jq * 128
                            o = oa_off(half, jq)
                            st_flag = first and jq == 0 and (half == 0 or dv != 64)
                            sc.op("pe", lambda: nc.tensor.matmul(
                                bank[:, o:o + dv + 1], lhsT=E[:, r, c0:c0 + 128], rhs=rhs,
                                start=st_flag, stop=last, skip_group_check=True),
                                reads=(("E", r), ("VA", ks, kt // 4)), writes=(bkey,), inc=(jq == 1))

                def evac_ops(g):
                    ops = []
                    tts = [g * 2, g * 2 + 1]
                    if dv == 64:
                        for half in range(2):
                            bank, bkey = oa_bank(g, half)
                            ov = bank[:, half * 256:(half + 1) * 256].rearrange("p (a b) -> p a b", a=2)
                            cols = slice(mix + half * 64, mix + half * 64 + 64)
                            rcs = RC[:, half * 4:half * 4 + 2]
                            rk = ("RC", half)
                            if kind == "odd":
                                h = 2 * p + half
                                ops.append(("dve", lambda ov=ov, rcs=rcs, h=h: nc.vector.tensor_scalar(
                                    out=rcs, in0=ov[:, :, 64], scalar1=ESINK[:, j, h:h + 1], scalar2=None,
                                    op0=ALU.add), (bkey, ("ESINK",)), (rk,)))
                                ops.append(("dve", lambda rcs=rcs: nc.vector.reciprocal(out=rcs, in_=rcs),
                                            (rk,), (rk,)))
                            else:
                                ops.append(("dve", lambda ov=ov, rcs=rcs: nc.vector.reciprocal(
                                    out=rcs, in_=ov[:, :, 64]), (bkey,), (rk,)))
                            for jq in range(2):
                                t = tts[jq]
                                ops.append(("dve", lambda ov=ov, jq=jq, t=t, cols=cols, half=half:
                                            nc.vector.scalar_tensor_tensor(
                                                out=YG[:, t, cols], in0=ov[:, jq, 0:64],
                                                scalar=RC[:, half * 4 + jq:half * 4 + jq + 1],
                                                in1=YG[:, t, cols], op0=ALU.mult, op1=ALU.mult),
                                            (bkey, rk, ("YG", t)), (("YG", t),)))
                        return ops
                    cols = slice(mix, mix + 128)
                    ova = OC[:, 0, :, :]
                    ovb = OC[:, 1, :, :]
                    bka = ("OC", 0)
                    bkb = ("OC", 1)
                    ops.append(("dve", lambda: nc.vector.reciprocal(out=RC[:, 0:2], in_=ova[:, :, 128]),
                                (bka,), (("RC", 0),)))
                    ops.append(("dve", lambda: nc.vector.reciprocal(out=RC[:, 4:6], in_=ovb[:, :, 128]),
                                (bkb,), (("RC", 1),)))
                    ops.append(("dve", lambda: nc.vector.tensor_scalar(
                        out=RC[:, 4:6], in0=RC[:, 4:6], scalar1=NEGLAM[:, j:j + 1], scalar2=None,
                        op0=ALU.mult), (("RC", 1), ("NEGLAM",)), (("RC", 1),)))
                    for jq in range(2):
                        ops.append(("dve", lambda jq=jq: nc.vector.tensor_scalar(
                            out=T1[:, jq, :], in0=ova[:, jq, 0:128], scalar1=RC[:, jq:jq + 1],
                            scalar2=None, op0=ALU.mult), (bka, ("RC", 0)), (("T1", jq),)))
                    for jq in range(2):
                        ops.append(("dve", lambda jq=jq: nc.vector.scalar_tensor_tensor(
                            out=T1[:, jq, :], in0=ovb[:, jq, 0:128], scalar=RC[:, 4 + jq:5 + jq],
                            in1=T1[:, jq, :], op0=ALU.mult, op1=ALU.add),
                            (bkb, ("RC", 1), ("T1", jq)), (("T1", jq),)))
                    ops.append(("dve", lambda: nc.vector.tensor_tensor(
                        out=T2[:, 0:2, :], in0=T1[:, 0:2, :], in1=T1[:, 0:2, :], op=ALU.mult),
                        (("T1", 0), ("T1", 1)), (("T2", 0), ("T2", 1))))
                    ops.append(("dve", lambda: nc.vector.tensor_reduce(
                        out=SQ[:, 0:2], in_=T2[:, 0:2, :], axis=AX.X, op=ALU.add),
                        (("T2", 0), ("T2", 1)), (("SQ",),)))
                    ops.append(("act", lambda: nc.scalar.activation(
                        out=SQ[:, 4:6], in_=SQ[:, 0:2], func=AF.Ln, scale=1.0 / 128, bias=EPSC[:, 0:1]),
                        (("SQ",), ("EPSC",)), (("SQ2",),)))
                    ops.append(("act", lambda: nc.scalar.activation(
                        out=SQ[:, 4:6], in_=SQ[:, 4:6], func=AF.Exp, scale=-0.5),
                        (("SQ2",),), (("SQ2",),)))
                    for jq in range(2):
                        t = tts[jq]
                        ops.append(("dve", lambda jq=jq, t=t: nc.vector.tensor_tensor(
                            out=T2[:, jq, :], in0=YG[:, t, cols], in1=SUBG[:, j, :], op=ALU.mult),
                            (("YG", t), ("SUBG",)), (("T2", jq),)))
                        ops.append(("dve", lambda jq=jq, t=t: nc.vector.scalar_tensor_tensor(
                            out=YG[:, t, cols], in0=T1[:, jq, :], scalar=SQ[:, 4 + jq:5 + jq],
                            in1=T2[:, jq, :], op0=ALU.mult, op1=ALU.mult),
                            (("T1", jq), ("T2", jq), ("SQ2",)), (("YG", t),)))
                    return ops

                def pop_pending(k):
                    for _ in range(k):
                        if not pending:
                            return
                        eng, fn, reads, writes = pending.pop(0)
                        sc.op(eng, fn, reads=reads, writes=writes)

                npop = 2 if per_group >= 8 else 4
                for i0 in range(min(_LOOK, n)):
                    emit_s(i0)
                hook_at = n // 2
                toks = {0: emit_softmax(0)}
                for idx in range(n):
                    if idx + 1 < n:
                        toks[idx + 1] = emit_softmax(idx + 1)
                    pop_pending(npop)
                    if idx + _LOOK < n:
                        emit_s(idx + _LOOK)
                    for _ in range(_NFILL):
                        sc.op("pe", lambda: nc.tensor.matmul(
                            ST4[:, 1536:1536 + 128], lhsT=IDB[:, :], rhs=IDB[:, :], start=True, stop=True),
                            reads=(("IDB",),), writes=(("DUM",),), inc=False)
                    emit_pv(idx, None)
                    g, kt, first, last = steps[idx]
                    if last:
                        if dv != 64:
                            for half in range(2):
                                bank, bkey = oa_bank(g, half)
                                sc.op("dve", lambda: nc.vector.tensor_copy(
                                    out=OC[:, half, :, 0:129],
                                    in_=bank[:, :].rearrange("p (a c) -> p a c", a=2)[:, :, 0:129]),
                                    reads=(bkey,), writes=(("OC", half),))
                        pending.extend(evac_ops(g))
                    if idx == hook_at and mid_hook is not None:
                        mid_hook()
                pop_pending(len(pending))

            npairs = 8
            proj_tables(0)
            proj(0)
            for p in range(npairs):
                nxt = p + 1 < npairs
                kp = pair_info(p)["kind"]
                mid_tables = nxt and pair_info(p + 1)["kind"] == kp and kp in ("dif", "odd")

                def hook(p=p, mid_tables=mid_tables):
                    if mid_tables:
                        proj_tables(p + 1)
                    proj(p + 1)
                attend(p, hook if nxt else None)
                if nxt and not mid_tables:
                    proj_tables(p + 1)

            transpose_to_HT(False, l)
            for c in range(4):
                slot = c % 2
                sc.dma("pool", "wb%d" % slot,
                       [(WB[:, slot, :, :], w_out_v[:, :, c * 256:(c + 1) * 256], (), (("WB", slot),))])
                for tp in range(NT // 2):
                    bank, bkey = scratch()
                    for i in range(2):
                        t = tp * 2 + i
                        for kc in range(KC):
                            sc.op("pe", lambda i=i, t=t, kc=kc: nc.tensor.matmul(
                                bank[:, i * 256:(i + 1) * 256],
                                lhsT=HT[:, kc, t * 128:(t + 1) * 128], rhs=WB[:, slot, kc, :],
                                start=(kc == 0), stop=(kc == KC - 1)),
                                reads=HTK(t // 4) + (("WB", slot),), writes=(bkey,),
                                inc=(i == 1 and kc == KC - 1))
                    ts_ = tp % 2
                    sc.op("dve", lambda tp=tp, ts_=ts_: nc.vector.tensor_tensor(
                        out=TMPO[:, ts_, :, :], in0=bank[:, :].rearrange("p (a b) -> p a b", a=2),
                        in1=GREP[:, c * 256:(c + 1) * 256].unsqueeze(1).broadcast_to([128, 2, 256]),
                        op=ALU.mult),
                        reads=(bkey, ("GREP", 2 * c), ("GREP", 2 * c + 1)), writes=(("TMPO", ts_),))
                    xs = X[:, tp * 2:tp * 2 + 2, c * 256:(c + 1) * 256]
                    sc.op("dve", lambda tp=tp, ts_=ts_, xs=xs: nc.vector.tensor_tensor(
                        out=xs, in0=xs, in1=TMPO[:, ts_, :, :], op=ALU.add),
                        reads=(("TMPO", ts_), ("X", tp * 2), ("X", tp * 2 + 1)),
                        writes=(("X", tp * 2), ("X", tp * 2 + 1)))

        for l in range(_NLAYERS):
            layer(l)

        sc.barrier()
        sc.dma("sp", "small", [(GREP[:, :], fg_d[0].partition_broadcast(128), (), tuple(("GREP", fc) for fc in range(KC)))])
        rms_stats(DEPTH)
        ov_ = out_d.rearrange("(t p) d -> p t d", p=128)
        out_toks = []
        for t in range(NT):
            slot = t % 2
            ob = OUTB[:, slot * 1024:(slot + 1) * 1024]
            sc.op("dve", lambda t=t, ob=ob: nc.vector.scalar_tensor_tensor(
                out=ob, in0=X[:, t, :], scalar=RSTD[:, t:t + 1], in1=GREP[:, :],
                op0=ALU.mult, op1=ALU.mult),
                reads=(("X", t), ("RSTD",)) + tuple(("GREP", fc) for fc in range(KC)), writes=(("OUTB", slot),))
            out_toks.append(sc.dma("sp", "out%d" % slot,
                                   [(ov_[:, t, :], ob, (("OUTB", slot),), ())]))
        sc._wait("sp", ("out0", sc.cnt["out0"]))
        sc._wait("sp", ("out1", sc.cnt["out1"]))
    _CACHE['sched'] = sc
    return nc


_CACHE = {}


def kernel(x, c, ada_w, ada_b, norm_g, ab_w_in, ab_w_out, diff_lq1, diff_lk1, diff_lq2,
           diff_lk2, diff_subln_g, c_w_in, c_w_out, c_sink, final_g):
    f = lambda a: np.ascontiguousarray(np.asarray(a, dtype=np.float32))
    x = f(x)
    c = f(c)
    if "nc" not in _CACHE:
        _CACHE["nc"] = build_nc()
        _CACHE["tabs"] = _tables()
    nc = _CACHE["nc"]
    tdil, tdif, todd = _CACHE["tabs"]
    adab = np.ascontiguousarray(f(ada_b).reshape(DEPTH, 24, 128).transpose(2, 0, 1))
    ng = np.ascontiguousarray(f(norm_g).reshape(DEPTH, 8, 128).transpose(2, 0, 1))
    shared = {
        "ada_w": f(ada_w), "adab": adab, "ng": ng,
        "ab_w_in": f(ab_w_in), "ab_w_out": f(ab_w_out),
        "lq1": f(diff_lq1), "lk1": f(diff_lk1), "lq2": f(diff_lq2), "lk2": f(diff_lk2),
        "subg": f(diff_subln_g), "c_w_in": f(c_w_in), "c_w_out": f(c_w_out),
        "c_sink": f(c_sink), "final_g": f(final_g).reshape(1, D),
        "t_dil": tdil, "t_dif": tdif, "t_odd": todd,
        "idb": np.eye(128, dtype=np.float32).astype(ml_dtypes.bfloat16),
        "idf": np.eye(128, dtype=np.float32),
    }
    in_maps = []
    for b in range(8):
        m = dict(shared)
        m["x"] = x[b]
        m["ct"] = np.ascontiguousarray(c[b].reshape(KC, 128).T)
        in_maps.append(m)
    res = run_bass_kernel_spmd(nc, in_maps, core_ids=list(range(8)))
    return np.stack([np.asarray(r["out"], dtype=np.float32) for r in res.results], axis=0)
```

```python
import math
from contextlib import ExitStack

import numpy as np
import ml_dtypes
import concourse.bass as bass
import concourse.mybir as mybir
from concourse.bass_utils import run_bass_kernel_spmd

F32 = mybir.dt.float32
BF16 = mybir.dt.bfloat16
AF = mybir.ActivationFunctionType
ALU = mybir.AluOpType
AX = mybir.AxisListType

S = 2048
D = 1024
NT = 16
KC = 8
DEPTH = 4
EPS = 1e-6
U_DIL, C_DIL = 2432, 1152
U_DIF, C_DIF = 3968, 1920
U_ODD, C_ODD = 640, 256
_NLAYERS = DEPTH
_NFILL = 0
_LOOK = 3


class Sched:
    def __init__(self, nc, es):
        self.nc = nc
        self.es = es
        self.eng = {"pe": nc.tensor, "act": nc.scalar, "dve": nc.vector,
                    "pool": nc.gpsimd, "sp": nc.sync}
        self.semh = {}
        self.cnt = {}
        for e in self.eng:
            self.semh[e] = es.enter_context(nc.semaphore("s_" + e))
            self.cnt[e] = 0
        self.seen = {}
        self.last_w = {}
        self.readers = {}
        self.trace = {e: [] for e in self.eng}

    def stream(self, name):
        if name not in self.semh:
            self.semh[name] = self.es.enter_context(self.nc.semaphore("d_" + name))
            self.cnt[name] = 0

    def _wait(self, eng, tok):
        if tok is None:
            return
        sk, v = tok
        if self.seen.get((eng, sk), 0) >= v:
            return
        self.eng[eng].wait_ge(self.semh[sk], v)
        self.trace[eng].append(("w", sk, v))
        self.seen[(eng, sk)] = v

    def _deps(self, eng, reads, writes):
        for k in reads:
            t = self.last_w.get(k)
            if t is not None and not (eng == "pe" and t[0] == "pe"):
                self._wait(eng, t)
        for k in writes:
            t = self.last_w.get(k)
            if t is not None and not (eng == "pe" and t[0] == "pe"):
                self._wait(eng, t)
            for sk, v in self.readers.get(k, {}).items():
                if not (eng == "pe" and sk == "pe"):
                    self._wait(eng, (sk, v))

    def _commit(self, tok, reads, writes):
        sk, v = tok
        for k in reads:
            r = self.readers.setdefault(k, {})
            if r.get(sk, 0) < v:
                r[sk] = v
        for k in writes:
            self.last_w[k] = tok
            self.readers[k] = {}

    def op(self, eng, fn, reads=(), writes=(), inc=True):
        self._deps(eng, reads, writes)
        inst = fn()
        if inc:
            self.cnt[eng] += 1
            inst.then_inc(self.semh[eng], 1)
            self.trace[eng].append(("i", eng, 1))
            tok = (eng, self.cnt[eng])
        else:
            tok = (eng, self.cnt[eng] + 1)
        self._commit(tok, reads, writes)
        return tok

    def dma(self, q, stream, items):
        self.stream(stream)
        for (o, i, reads, writes) in items:
            self._deps(q, reads, writes)
        for (o, i, reads, writes) in items:
            inst = self.eng[q].dma_start(out=o, in_=i)
            self.cnt[stream] += 16
            inst.then_inc(self.semh[stream], 16)
            self.trace[q].append(("i", stream, 16))
        tok = (stream, self.cnt[stream])
        for (o, i, reads, writes) in items:
            self._commit(tok, reads, writes)
        return tok

    def barrier(self):
        for e in self.eng:
            for sk in list(self.semh.keys()):
                if sk != e and self.cnt[sk] > 0:
                    self._wait(e, (sk, self.cnt[sk]))


def _slopes(n):
    return 2.0 ** (-8.0 * np.arange(1, n + 1, dtype=np.float64) / n)


def _toeplitz(U, C, fn):
    i = np.arange(128, dtype=np.int64)[:, None]
    u = np.arange(U, dtype=np.int64)[None, :]
    d = np.abs(u - i - C)
    return fn(d).astype(np.float32).astype(ml_dtypes.bfloat16)


def _tables():
    sd = _slopes(8)
    dil = np.stack([
        _toeplitz(U_DIL, C_DIL, lambda d, s=s: (
            (d <= 64).astype(np.float64)
            + ((d % 4 == 0) & (d <= 256)).astype(np.float64)
            + ((d % 16 == 0) & (d <= 1024)).astype(np.float64)) * np.exp(-s * d))
        for s in sd])
    sf = _slopes(4)
    dif = np.stack([_toeplitz(U_DIF, C_DIF, lambda d, s=s: np.exp(-s * d)) for s in sf])
    so = _slopes(16)
    odd = np.stack([
        _toeplitz(U_ODD, C_ODD, lambda d, s=s: (d <= 128).astype(np.float64) * np.exp(-s * d))
        for s in so])
    return dil, dif, odd


def build_nc():
    if "tabs" not in _CACHE:
        _CACHE["tabs"] = _tables()
    np_dil, np_dif, np_odd = [np.asarray(t).astype(np.float32) != 0 for t in _CACHE["tabs"]]
    nc = bass.Bass("TRN2", target_bir_lowering=False)
    dram = lambda n, sh, dt, kind="ExternalInput": nc.dram_tensor(n, sh, dt, kind=kind)
    x_d = dram("x", [S, D], F32).ap()
    ct_d = dram("ct", [128, KC], F32).ap()
    adaw_d = dram("ada_w", [DEPTH, D, 3 * D], F32).ap()
    adab_d = dram("adab", [128, DEPTH, 24], F32).ap()
    ng_d = dram("ng", [128, DEPTH, 8], F32).ap()
    abin_d = dram("ab_w_in", [2, D, 4096], F32).ap()
    about_d = dram("ab_w_out", [2, D, D], F32).ap()
    lq1_d = dram("lq1", [2, 64], F32).ap()
    lk1_d = dram("lk1", [2, 64], F32).ap()
    lq2_d = dram("lq2", [2, 64], F32).ap()
    lk2_d = dram("lk2", [2, 64], F32).ap()
    subg_d = dram("subg", [2, 128], F32).ap()
    cin_d = dram("c_w_in", [2, D, 2560], F32).ap()
    cout_d = dram("c_w_out", [2, D, D], F32).ap()
    sink_d = dram("c_sink", [2, 16], F32).ap()
    fg_d = dram("final_g", [1, D], F32).ap()
    tdil_d = dram("t_dil", [8, 128, U_DIL], BF16).ap()
    tdif_d = dram("t_dif", [4, 128, U_DIF], BF16).ap()
    todd_d = dram("t_odd", [16, 128, U_ODD], BF16).ap()
    idb_d = dram("idb", [128, 128], BF16).ap()
    idf_d = dram("idf", [128, 128], F32).ap()
    out_d = dram("out", [S, D], F32, kind="ExternalOutput").ap()

    es = ExitStack()
    with es:
        sb = lambda n, sh, dt: es.enter_context(nc.sbuf_tensor(n, sh, dt))
        X = sb("X", [128, NT, D], F32)
        HT = sb("HT", [128, KC, S], BF16)
        YG = sb("YG", [128, NT, D], BF16)
        WB = sb("WB", [128, 2, KC, 256], BF16)
        QT = sb("QT", [128, 2, S], BF16)
        KZ = sb("KZ", [128, 2, 2, S], BF16)
        WP = sb("WP", [128, KC, 384], BF16)
        VA = sb("VA", [128, 2, NT * 130], BF16)
        E = sb("E", [128, 3, 512], BF16)
        TOE = sb("TOE", [128, 2, U_DIF], BF16)
        GREP = sb("GREP", [128, D], F32)
        IDB = sb("IDB", [128, 128], BF16)
        IDF = sb("IDF", [128, 128], F32)
        MOD = sb("MOD", [128, DEPTH, 24], F32)
        CS = sb("CS", [128, KC], F32)
        SS = sb("SS", [128, NT], F32)
        RSTD = sb("RSTD", [128, NT], F32)
        SUBG = sb("SUBG", [128, 2, 128], F32)
        LSUM = sb("LSUM", [128, 4], F32)
        NEGLAM = sb("NEGLAM", [128, 2], F32)
        ESINK = sb("ESINK", [128, 2, 16], F32)
        RC = sb("RC", [128, 8], F32)
        T1 = sb("T1", [128, 2, 128], F32)
        T2 = sb("T2", [128, 2, 128], F32)
        OC = sb("OC", [128, 2, 2, 132], F32)
        SQ = sb("SQ", [128, 8], F32)
        TMPO = sb("TMPO", [128, 2, 2, 256], F32)
        EPSC = sb("EPSC", [128, 1], F32)

        pst = lambda n: es.enter_context(nc.psum_tensor(n, [128, 512], F32))
        ST4 = es.enter_context(nc.psum_tensor("ST4", [128, 2048], F32))
        OA = [pst("OA%d" % i) for i in range(2)]
        MISCS = [pst("MISC%d" % i) for i in range(2)]
        SCR = MISCS + OA
        SCRK = [("MISC", 0), ("MISC", 1), ("OA", 0), ("OA", 1)]
        misc_i = [0]

        def misc():
            i = misc_i[0] % 2
            misc_i[0] += 1
            return MISCS[i], ("MISC", i)

        sc = Sched(nc, es)
        WBf = WB[:, :, :, :].rearrange("p a b c -> p (a b c)").bitcast(F32)
        LQK = WBf[:, 0:512].rearrange("p (a b c) -> p a b c", a=4, b=2)
        LPR = WBf[:, 512:768].rearrange("p (a b c) -> p a b c", a=2, b=2)
        ADAB = WBf[:, 768:864].rearrange("p (a b) -> p a b", a=DEPTH)
        NG = WBf[:, 864:896].rearrange("p (a b) -> p a b", a=DEPTH)
        scr_i = [0]

        def scratch():
            i = scr_i[0] % 4
            scr_i[0] += 1
            return SCR[i], SCRK[i]

        HTflat = HT[:, :, :].rearrange("p a b -> p (a b)")
        AW = HTflat[:, 0:12288].bitcast(F32)
        OUTB = HTflat[:, 0:4096].bitcast(F32)
        HTK = lambda tg: tuple(("HT", tg, fc) for fc in range(KC))

        xv = x_d.rearrange("(t p) d -> p t d", p=128)
        sc.dma("sp", "xload", [(X[:, t, :], xv[:, t, :], (), (("X", t),)) for t in range(NT)])
        small = [
            (CS[:, :], ct_d, (), (("CS",),)),
            (ADAB[:, :, :], adab_d, (), (("ADAB",),)),
            (NG[:, :, :], ng_d, (), (("NG",),)),
            (IDB[:, :], idb_d, (), (("IDB",),)),
            (IDF[:, :], idf_d, (), (("IDF",),)),
        ]
        for j in range(2):
            small.append((SUBG[:, j, :], subg_d[j].partition_broadcast(128), (), (("SUBG",),)))
            for qi, dd in enumerate((lq1_d, lk1_d, lq2_d, lk2_d)):
                small.append((LQK[:, qi, j, :], dd[j].partition_broadcast(128), (), (("LQK",),)))
            small.append((ESINK[:, j, :], sink_d[j].partition_broadcast(128), (), (("ESINK",),)))
        sc.dma("sp", "small", small)

        sc.op("act", lambda: nc.scalar.activation(out=CS[:, :], in_=CS[:, :], func=AF.Silu),
              reads=(("CS",),), writes=(("CS",),))
        sc.op("dve", lambda: nc.vector.memset(MOD[:, :, :], 0.0), writes=(("MOD",),))
        sc.op("dve", lambda: nc.vector.memset(EPSC[:, :], EPS), writes=(("EPSC",),))

        YGflat = YG[:, :, :].rearrange("p a b -> p (a b)")
        ACC3 = YGflat[:, 0:6144].bitcast(F32)
        ONESF = YGflat[:, 6144:6146].bitcast(F32)
        QKflat = KZ[:, :, :, :].rearrange("p a b c -> p (a b c)")
        AWS = [AW[:, 0:3072], AW[:, 3072:6144], YGflat[:, 8192:8192 + 6144].bitcast(F32),
               QKflat[:, 0:6144].bitcast(F32), TOE[:, :, :].rearrange("p a b -> p (a b)")[:, 0:6144].bitcast(F32)]
        sc.op("dve", lambda: nc.vector.memset(ONESF, 1.0), writes=(("ONESF",),))
        CSR = VA[:, :, :].rearrange("p a b -> p (a b)")[:, 0:2048].bitcast(F32).rearrange(
            "p (a b) -> p a b", a=KC)
        sc.op("dve", lambda: nc.vector.tensor_copy(
            out=CSR, in_=CS[:, :].unsqueeze(2).broadcast_to([128, KC, 128])),
            reads=(("CS",),), writes=(("CSR",),))
        GP = ST4[:, 0:1024]
        TP = ST4[:, 1024:2048]
        awi = 0
        for l in range(DEPTH):
            for kc in range(KC):
                slot = awi % 5
                awq = ("sp", "act", "pool")[awi % 3]
                awi += 1
                awv = AWS[slot]
                sc.dma(awq, "aw%d" % slot,
                       [(awv, adaw_d[l, kc * 128:(kc + 1) * 128, :], (), (("AW", slot),))])
                if kc == 0:
                    sc.op("dve", lambda: nc.vector.tensor_scalar(
                        out=ACC3[:, 0:2048], in0=awv[:, 0:2048], scalar1=CS[:, kc:kc + 1], scalar2=None,
                        op0=ALU.mult),
                        reads=(("AW", slot), ("CS",)), writes=(("ACC3",),))
                else:
                    sc.op("dve", lambda: nc.vector.scalar_tensor_tensor(
                        out=ACC3[:, 0:2048], in0=awv[:, 0:2048], scalar=CS[:, kc:kc + 1], in1=ACC3[:, 0:2048],
                        op0=ALU.mult, op1=ALU.add),
                        reads=(("AW", slot), ("CS",), ("ACC3",)), writes=(("ACC3",),))
                for cg in range(2):
                    sc.op("pe", lambda cg=cg: nc.tensor.matmul(
                        GP[:, cg * 512:(cg + 1) * 512], lhsT=CSR[:, kc, :],
                        rhs=awv[:, 2048 + cg * 512:2048 + (cg + 1) * 512],
                        start=(kc == 0), stop=(kc == KC - 1)),
                        reads=(("AW", slot), ("CSR",)), writes=(("GP", cg),), inc=(cg == 1))
            bank, bkey = scratch()
            for jc in range(16):
                sc.op("pe", lambda jc=jc: nc.tensor.matmul(
                    bank[:, jc:jc + 1], lhsT=ACC3[:, jc * 128:(jc + 1) * 128],
                    rhs=ONESF, start=True, stop=True),
                    reads=(("ACC3",), ("ONESF",)), writes=(bkey,), inc=(jc == 15))
            sc.op("dve", lambda: nc.vector.tensor_copy(out=MOD[:, l, 0:16], in_=bank[:, 0:16]),
                  reads=(bkey,), writes=(("MOD",),))
            sc.op("act", lambda: nc.scalar.copy(out=GREP[:, :], in_=GP),
                  reads=(("GP", 0), ("GP", 1)), writes=tuple(("GREP", fc) for fc in range(KC)))
            for fc in range(KC):
                sc.op("pe", lambda fc=fc: nc.tensor.transpose(
                    TP[:, fc * 128:(fc + 1) * 128], GREP[:, fc * 128:(fc + 1) * 128], IDF[:, :]),
                    reads=(("GREP", fc), ("IDF",)), writes=(("TP",),), inc=(fc == KC - 1))
            sc.op("dve", lambda: nc.vector.tensor_copy(
                out=MOD[:, l, 16:24], in_=TP.rearrange("p (a b) -> p a b", a=KC)[:, :, 0]),
                reads=(("TP",),), writes=(("MOD",),))
        sc.op("dve", lambda: nc.vector.tensor_tensor(
            out=MOD[:, :, :], in0=MOD[:, :, :], in1=ADAB[:, :, :], op=ALU.add),
            reads=(("MOD",), ("ADAB",)), writes=(("MOD",),))
        sc.op("dve", lambda: nc.vector.scalar_tensor_tensor(
            out=MOD[:, :, 8:16], in0=MOD[:, :, 8:16], scalar=1.0, in1=NG[:, :, :],
            op0=ALU.add, op1=ALU.mult),
            reads=(("MOD",), ("NG",)), writes=(("MOD",),))

        sc.op("dve", lambda: nc.vector.tensor_tensor(
            out=LPR[:, 0, :, :], in0=LQK[:, 0, :, :], in1=LQK[:, 1, :, :], op=ALU.mult),
            reads=(("LQK",),), writes=(("LPR",),))
        sc.op("dve", lambda: nc.vector.tensor_tensor(
            out=LPR[:, 1, :, :], in0=LQK[:, 2, :, :], in1=LQK[:, 3, :, :], op=ALU.mult),
            reads=(("LQK",),), writes=(("LPR",),))
        sc.op("dve", lambda: nc.vector.tensor_reduce(
            out=LSUM[:, :], in_=LPR[:, :, :, :].rearrange("p a b c -> p (a b) c"),
            axis=AX.X, op=ALU.add),
            reads=(("LPR",),), writes=(("LSUM",),))
        sc.op("act", lambda: nc.scalar.activation(out=LSUM[:, :], in_=LSUM[:, :], func=AF.Exp),
              reads=(("LSUM",),), writes=(("LSUM",),))
        sc.op("dve", lambda: nc.vector.tensor_tensor(
            out=NEGLAM[:, :], in0=LSUM[:, 2:4], in1=LSUM[:, 0:2], op=ALU.subtract),
            reads=(("LSUM",),), writes=(("NEGLAM",),))
        lam_inits = [0.8 - 0.6 * math.exp(-0.3 * 0), 0.8 - 0.6 * math.exp(-0.3 * 2)]
        for j in range(2):
            sc.op("dve", lambda j=j: nc.vector.tensor_scalar(
                out=NEGLAM[:, j:j + 1], in0=NEGLAM[:, j:j + 1], scalar1=-lam_inits[j],
                scalar2=None, op0=ALU.add),
                reads=(("NEGLAM",),), writes=(("NEGLAM",),))
            sc.op("dve", lambda j=j: nc.vector.tensor_scalar(
                out=SUBG[:, j, :], in0=SUBG[:, j, :], scalar1=1.0 - lam_inits[j],
                scalar2=None, op0=ALU.mult),
                reads=(("SUBG",),), writes=(("SUBG",),))
        sc.op("act", lambda: nc.scalar.activation(out=ESINK[:, :, :], in_=ESINK[:, :, :], func=AF.Exp),
              reads=(("ESINK",),), writes=(("ESINK",),))
        def rms_stats(l):
            sc.op("dve", lambda: nc.vector.memset(SS[:, :], 0.0), writes=tuple(("SS", t) for t in range(NT)))
            for t in range(NT):
                junk = TMPO[:, t % 2, :, :].rearrange("p a b -> p (a b)").bitcast(BF16)
                if t % 2 == 0:
                    sc.op("act", lambda t=t, junk=junk: nc.scalar.activation(
                        out=junk, in_=X[:, t, :], func=AF.Square, accum_out=SS[:, t:t + 1]),
                        reads=(("X", t),), writes=(("TMPO", t % 2), ("SS", t)))
                else:
                    sc.op("dve", lambda t=t, junk=junk: nc.vector.scalar_tensor_tensor(
                        out=junk, in0=X[:, t, :], scalar=1.0, in1=X[:, t, :], op0=ALU.mult, op1=ALU.mult,
                        accum_out=SS[:, t:t + 1]),
                        reads=(("X", t),), writes=(("TMPO", t % 2), ("SS", t)))
            sc.op("act", lambda: nc.scalar.activation(
                out=RSTD[:, :], in_=SS[:, :], func=AF.Ln, scale=1.0 / D, bias=EPSC[:, 0:1]),
                reads=tuple(("SS", t) for t in range(NT)) + (("EPSC",),), writes=(("RSTD",),))
            sc.op("act", lambda: nc.scalar.activation(
                out=RSTD[:, :], in_=RSTD[:, :], func=AF.Exp, scale=-0.5),
                reads=(("RSTD",),), writes=(("RSTD",),))

        rms_stats(0)
        sc.barrier()
        sc.op("pool", lambda: nc.gpsimd.memset(QKflat, 0.0),
              writes=tuple(("K", sl, tg) for sl in range(2) for tg in range(4)))

        def transpose_to_HT(src_is_xn, l):
            for tg in range(4):
                for fc in range(KC):
                    bank, bkey = scratch()
                    bb = bank[:, :].bitcast(BF16)
                    for i in range(4):
                        t = tg * 4 + i
                        sc.op("pe", lambda i=i, t=t: nc.tensor.transpose(
                            bb[:, i * 128:(i + 1) * 128], YG[:, t, fc * 128:(fc + 1) * 128], IDB[:, :]),
                            reads=(("YG", t), ("IDB",)), writes=(bkey,), inc=(i == 3))
                    dst = HT[:, fc, tg * 512:(tg + 1) * 512]
                    if src_is_xn:
                        sc.op("dve", lambda: nc.vector.tensor_scalar(
                            out=dst, in0=bb[:, 0:512], scalar1=MOD[:, l, 8 + fc:9 + fc],
                            scalar2=MOD[:, l, fc:fc + 1], op0=ALU.mult, op1=ALU.add),
                            reads=(bkey, ("MOD",)), writes=(("HT", tg, fc),))
                    else:
                        sc.op("act", lambda: nc.scalar.copy(out=dst, in_=bb[:, 0:512]),
                              reads=(bkey,), writes=(("HT", tg, fc),))

        def layer(l):
            even = (l % 2 == 0)
            j = l // 2
            w_in = abin_d[j] if even else cin_d[j]
            w_out = about_d[j] if even else cout_d[j]
            gcol0 = 3072 if even else 1536
            w_in_v = w_in.rearrange("(kc p) n -> p kc n", p=128)
            w_out_v = w_out.rearrange("(kc p) n -> p kc n", p=128)

            if l > 0:
                rms_stats(l)
            for t in range(NT):
                sc.op("dve", lambda t=t: nc.vector.tensor_scalar(
                    out=YG[:, t, :], in0=X[:, t, :], scalar1=RSTD[:, t:t + 1], scalar2=None,
                    op0=ALU.mult),
                    reads=(("X", t), ("RSTD",)), writes=(("YG", t),))
            for fc in range(KC):
                bank, bkey = scratch()
                sc.op("pe", lambda fc=fc: nc.tensor.matmul(
                    bank[:, 0:128], lhsT=MOD[:, l, 16 + fc:17 + fc].broadcast_to([128, 128]),
                    rhs=IDF[:, :], start=True, stop=True),
                    reads=(("MOD",), ("IDF",)), writes=(bkey,))
                sc.op("act", lambda fc=fc: nc.scalar.copy(
                    out=GREP[:, fc * 128:(fc + 1) * 128], in_=bank[:, 0:128]),
                    reads=(bkey,), writes=(("GREP", fc),))
            transpose_to_HT(True, l)

            for c in range(4):
                slot = c % 2
                sc.dma("pool", "wb%d" % slot,
                       [(WB[:, slot, :, :], w_in_v[:, :, gcol0 + c * 256: gcol0 + (c + 1) * 256],
                         (), (("WB", slot),))])
                for tp in range(NT // 2):
                    bank, bkey = scratch()
                    for i in range(2):
                        t = tp * 2 + i
                        for kc in range(KC):
                            sc.op("pe", lambda i=i, t=t, kc=kc: nc.tensor.matmul(
                                bank[:, i * 256:(i + 1) * 256],
                                lhsT=HT[:, kc, t * 128:(t + 1) * 128], rhs=WB[:, slot, kc, :],
                                start=(kc == 0), stop=(kc == KC - 1)),
                                reads=HTK(t // 4) + (("WB", slot),), writes=(bkey,),
                                inc=(i == 1 and kc == KC - 1))
                    sc.op("act", lambda tp=tp: nc.scalar.activation(
                        out=YG[:, tp * 2:tp * 2 + 2, c * 256:(c + 1) * 256],
                        in_=bank[:, :].rearrange("p (a b) -> p a b", a=2), func=AF.Silu),
                        reads=(bkey,), writes=(("YG", tp * 2), ("YG", tp * 2 + 1)))

            def pair_info(p):
                if even and p < 4:
                    return dict(kind="dil", q0=128 * p, k0=512 + 128 * p, kw=128, v0=1024 + 128 * p,
                                vw=128, dv=64, mix=128 * p,
                                tabs=[(tdil_d[2 * p], U_DIL), (tdil_d[2 * p + 1], U_DIL)], C=C_DIL,
                                nz=[np_dil[2 * p], np_dil[2 * p + 1]], dmin=-1152, dmax=1024)
                if even:
                    jh = p - 4
                    return dict(kind="dif", q0=1536 + 128 * jh, k0=2048 + 128 * jh, kw=128,
                                v0=2560 + 128 * jh, vw=128, dv=128, mix=512 + 128 * jh,
                                tabs=[(tdif_d[jh], U_DIF)], C=C_DIF, nz=[np_dif[jh]],
                                dmin=-1920, dmax=1792)
                g = p // 2
                return dict(kind="odd", q0=128 * p, k0=1024 + 64 * g, kw=64, v0=1280 + 64 * g,
                            vw=64, dv=64, mix=128 * p,
                            tabs=[(todd_d[2 * p], U_ODD), (todd_d[2 * p + 1], U_ODD)], C=C_ODD,
                            nz=[np_odd[2 * p], np_odd[2 * p + 1]], dmin=-256, dmax=128)

            def va_view(slot, info):
                if info["kind"] == "dil":
                    return VA[:, slot, 0:NT * 130].rearrange("p (t h c) -> p t h c", t=NT, h=2)
                if info["kind"] == "dif":
                    return VA[:, slot, 0:NT * 129].rearrange("p (t c) -> p t c", t=NT)
                return VA[:, slot, 0:NT * 65].rearrange("p (t c) -> p t c", t=NT)

            def kv_slot(p):
                return (p // 2) % 2 if not even else p % 2

            def need_kv(p):
                return even or p % 2 == 0

            def proj(p):
                info = pair_info(p)
                slot = p % 2
                items = [(WP[:, :, 0:128], w_in_v[:, :, info["q0"]:info["q0"] + 128], (), (("WP",),))]
                if not need_kv(p):
                    pass
                elif info["kw"] == 128:
                    items.append((WP[:, :, 128:256], w_in_v[:, :, info["k0"]:info["k0"] + 128], (), (("WP",),)))
                else:
                    items.append((WP[:, :, 128:192], w_in_v[:, :, info["k0"]:info["k0"] + 64], (), (("WP",),)))
                    items.append((WP[:, :, 192:256], w_in_v[:, :, info["k0"]:info["k0"] + 64], (), (("WP",),)))
                if need_kv(p):
                    items.append((WP[:, :, 256:256 + info["vw"]],
                                  w_in_v[:, :, info["v0"]:info["v0"] + info["vw"]], (), (("WP",),)))
                sc.dma("pool", "wp", items)
                proj_rest(p, info, slot)

            def proj_tables(p):
                info = pair_info(p)
                slot = p % 2
                if info["kind"] == "dif":
                    tab, U = info["tabs"][0]
                    sc.dma("sp", "toe%d" % slot, [(TOE[:, slot, 0:U], tab, (), (("TOE", slot),))])
                elif info["kind"] == "dil":
                    for h in range(2):
                        tab, U = info["tabs"][h]
                        sc.dma("sp", "toe%d" % h, [(TOE[:, h, 0:U], tab, (), (("TOE", h),))])
                else:
                    sc.dma("sp", "toe%d" % slot,
                           [(TOE[:, slot, h * U_ODD:(h + 1) * U_ODD], info["tabs"][h][0], (), (("TOE", slot),))
                            for h in range(2)])

            def proj_rest(p, info, slot):
                ks = kv_slot(p)
                for which in range(2):
                    if which == 1 and not need_kv(p):
                        continue
                    dslot = slot if which == 0 else ks
                    for tg in range(4):
                        MISC, mkey = misc()
                        for kc in range(KC):
                            sc.op("pe", lambda kc=kc, tg=tg, which=which: nc.tensor.matmul(
                                MISC[:, :], lhsT=WP[:, kc, which * 128:(which + 1) * 128],
                                rhs=HT[:, kc, tg * 512:(tg + 1) * 512],
                                start=(kc == 0), stop=(kc == KC - 1)),
                                reads=(("WP",),) + HTK(tg), writes=(mkey,), inc=(kc == KC - 1))
                        if which == 0:
                            sc.op("dve", lambda tg=tg: nc.vector.tensor_copy(
                                out=QT[:, dslot, tg * 512:(tg + 1) * 512], in_=MISC[:, :]),
                                reads=(mkey,), writes=(("Q", dslot, tg),))
                        else:
                            for hh in range(2):
                                rws = slice(hh * 64, hh * 64 + 64)
                                sc.op("dve", lambda tg=tg, hh=hh, rws=rws: nc.vector.tensor_copy(
                                    out=KZ[rws, dslot, hh, tg * 512:(tg + 1) * 512], in_=MISC[rws, :]),
                                    reads=(mkey,), writes=(("K", dslot, tg),))
                if not need_kv(p):
                    return
                slot = ks
                vv = va_view(slot, info)
                vw = info["vw"]
                if info["kind"] == "dil":
                    sc.op("pool", lambda: nc.gpsimd.memset(vv[:, :, :, 64:65], 1.0), writes=tuple(("VA", slot, q) for q in range(4)))
                elif info["kind"] == "dif":
                    sc.op("pool", lambda: nc.gpsimd.memset(vv[:, :, 128:129], 1.0), writes=tuple(("VA", slot, q) for q in range(4)))
                else:
                    sc.op("pool", lambda: nc.gpsimd.memset(vv[:, :, 64:65], 1.0), writes=tuple(("VA", slot, q) for q in range(4)))
                for tg in range(4):
                    MISC, mkey = misc()
                    for i in range(4):
                        t = tg * 4 + i
                        for kc in range(KC):
                            sc.op("pe", lambda kc=kc, t=t, i=i: nc.tensor.matmul(
                                MISC[:, i * 128:i * 128 + vw],
                                lhsT=HT[:, kc, t * 128:(t + 1) * 128], rhs=WP[:, kc, 256:256 + vw],
                                start=(kc == 0), stop=(kc == KC - 1)),
                                reads=(("WP",),) + HTK(tg), writes=(mkey,),
                                inc=(i == 3 and kc == KC - 1))
                    src = MISC[:, :].rearrange("p (a b) -> p a b", a=4)
                    if info["kind"] == "dil":
                        sc.op("dve", lambda tg=tg: nc.vector.tensor_copy(
                            out=vv[:, tg * 4:tg * 4 + 4, :, 0:64],
                            in_=src.rearrange("p a (h c) -> p a h c", h=2)),
                            reads=(mkey,), writes=(("VA", slot, tg),))
                    elif info["kind"] == "dif":
                        sc.op("dve", lambda tg=tg: nc.vector.tensor_copy(
                            out=vv[:, tg * 4:tg * 4 + 4, 0:128], in_=src),
                            reads=(mkey,), writes=(("VA", slot, tg),))
                    else:
                        sc.op("dve", lambda tg=tg: nc.vector.tensor_copy(
                            out=vv[:, tg * 4:tg * 4 + 4, 0:64], in_=src[:, :, 0:64]),
                            reads=(mkey,), writes=(("VA", slot, tg),))

            def attend(p, mid_hook):
                info = pair_info(p)
                slot = p % 2
                kind = info["kind"]
                dv = info["dv"]
                st = 256
                ks = kv_slot(p)
                vv = va_view(ks, info)
                mix = info["mix"]
                steps = []
                per_group = 0
                for g in range(8):
                    kts = []
                    for kt in range(NT):
                        dlt = g * 256 - kt * 128
                        if not (info["dmin"] <= dlt <= info["dmax"]):
                            continue
                        o_ = dlt + info["C"]
                        if not any(z[:, o_:o_ + 256].any() for z in info["nz"]):
                            continue
                        kts.append(kt)
                    per_group = max(per_group, len(kts))
                    for i, kt in enumerate(kts):
                        steps.append((g, kt, i == 0, i == len(kts) - 1))
                n = len(steps)
                pending = []

                def oa_bank(g, half):
                    b = (g % 2) if dv == 64 else half
                    return OA[b], ("OA", b)

                def oa_off(half, jq):
                    return (half * 2 + jq) * 128 if dv == 64 else jq * 256

                def st_loc(idx):
                    return idx % 2, 0

                def emit_s(idx):
                    g, kt, first, last = steps[idx]
                    b = idx % 4
                    for half in range(2):
                        c0 = b * 512 + half * 256
                        sc.op("pe", lambda: nc.tensor.matmul(
                            ST4[:, c0:c0 + 256],
                            lhsT=KZ[:, ks, half, kt * 128:(kt + 1) * 128],
                            rhs=QT[:, slot, g * 256:(g + 1) * 256], start=True, stop=True),
                            reads=(("Q", slot, g // 2), ("K", ks, kt // 4)),
                            writes=(("ST", b),), inc=(half == 1))

                def emit_softmax(idx):
                    g, kt, first, last = steps[idx]
                    r = idx % 3
                    b = idx % 4
                    atok = sc.op("act", lambda: nc.scalar.activation(
                        out=E[:, r, :], in_=ST4[:, b * 512:(b + 1) * 512], func=AF.Exp, scale=0.125),
                        reads=(("ST", b),), writes=(("E", r),))
                    off = g * 256 - kt * 128 + info["C"]
                    ev = E[:, r, :].rearrange("p (h q) -> p h q", h=2)
                    if kind == "dif":
                        tab = TOE[:, slot, off:off + 256].unsqueeze(1).broadcast_to([128, 2, 256])
                        tkeys = (("TOE", slot),)
                    elif kind == "dil":
                        tab = TOE[:, 0:2, off:off + 256]
                        tkeys = (("TOE", 0), ("TOE", 1))
                    else:
                        tab = TOE[:, slot, 0:2 * U_ODD].rearrange("p (h u) -> p h u", h=2)[:, :, off:off + 256]
                        tkeys = (("TOE", slot),)
                    dtok = sc.op("dve", lambda: nc.vector.tensor_tensor(out=ev, in0=ev, in1=tab, op=ALU.mult),
                                 reads=(("E", r),) + tkeys, writes=(("E", r),))
                    return atok, dtok

                def emit_pv(idx, hoist_tok=None):
                    g, kt, first, last = steps[idx]
                    r = idx % 3
                    for half in range(2):
                        bank, bkey = oa_bank(g, half)
                        rhs = vv[:, kt, half, :] if kind == "dil" else vv[:, kt, :]
                        for jq in range(2):
                            if half == 1 and jq == 1 and hoist_tok is not None:
                                sc._wait("pe", hoist_tok)
                            c0 = half * 256 + jq * 128
                            o = oa_off(half, jq)
                            st_flag = first and jq == 0 and (half == 0 or dv != 64)
                            sc.op("pe", lambda: nc.tensor.matmul(
                                bank[:, o:o + dv + 1], lhsT=E[:, r, c0:c0 + 128], rhs=rhs,
                                start=st_flag, stop=last, skip_group_check=True),
                                reads=(("E", r), ("VA", ks, kt // 4)), writes=(bkey,), inc=(jq == 1))

                def evac_ops(g):
                    ops = []
                    tts = [g * 2, g * 2 + 1]
                    if dv == 64:
                        for half in range(2):
                            bank, bkey = oa_bank(g, half)
                            ov = bank[:, half * 256:(half + 1) * 256].rearrange("p (a b) -> p a b", a=2)
                            cols = slice(mix + half * 64, mix + half * 64 + 64)
                            rcs = RC[:, half * 4:half * 4 + 2]
                            rk = ("RC", half)
                            if kind == "odd":
                                h = 2 * p + half
                                ops.append(("dve", lambda ov=ov, rcs=rcs, h=h: nc.vector.tensor_scalar(
                                    out=rcs, in0=ov[:, :, 64], scalar1=ESINK[:, j, h:h + 1], scalar2=None,
                                    op0=ALU.add), (bkey, ("ESINK",)), (rk,)))
                                ops.append(("dve", lambda rcs=rcs: nc.vector.reciprocal(out=rcs, in_=rcs),
                                            (rk,), (rk,)))
                            else:
                                ops.append(("dve", lambda ov=ov, rcs=rcs: nc.vector.reciprocal(
                                    out=rcs, in_=ov[:, :, 64]), (bkey,), (rk,)))
                            for jq in range(2):
                                t = tts[jq]
                                ops.append(("dve", lambda ov=ov, jq=jq, t=t, cols=cols, half=half:
                                            nc.vector.scalar_tensor_tensor(
                                                out=YG[:, t, cols], in0=ov[:, jq, 0:64],
                                                scalar=RC[:, half * 4 + jq:half * 4 + jq + 1],
                                                in1=YG[:, t, cols], op0=ALU.mult, op1=ALU.mult),
                                            (bkey, rk, ("YG", t)), (("YG", t),)))
                        return ops
                    cols = slice(mix, mix + 128)
                    ova = OC[:, 0, :, :]
                    ovb = OC[:, 1, :, :]
                    bka = ("OC", 0)
                    bkb = ("OC", 1)
                    ops.append(("dve", lambda: nc.vector.reciprocal(out=RC[:, 0:2], in_=ova[:, :, 128]),
                                (bka,), (("RC", 0),)))
                    ops.append(("dve", lambda: nc.vector.reciprocal(out=RC[:, 4:6], in_=ovb[:, :, 128]),
                                (bkb,), (("RC", 1),)))
                    ops.append(("dve", lambda: nc.vector.tensor_scalar(
                        out=RC[:, 4:6], in0=RC[:, 4:6], scalar1=NEGLAM[:, j:j + 1], scalar2=None,
                        op0=ALU.mult), (("RC", 1), ("NEGLAM",)), (("RC", 1),)))
                    for jq in range(2):
                        ops.append(("dve", lambda jq=jq: nc.vector.tensor_scalar(
                            out=T1[:, jq, :], in0=ova[:, jq, 0:128], scalar1=RC[:, jq:jq + 1],
                            scalar2=None, op0=ALU.mult), (bka, ("RC", 0)), (("T1", jq),)))
                    for jq in range(2):
                        ops.append(("dve", lambda jq=jq: nc.vector.scalar_tensor_tensor(
                            out=T1[:, jq, :], in0=ovb[:, jq, 0:128], scalar=RC[:, 4 + jq:5 + jq],
                            in1=T1[:, jq, :], op0=ALU.mult, op1=ALU.add),
                            (bkb, ("RC", 1), ("T1", jq)), (("T1", jq),)))
                    ops.append(("dve", lambda: nc.vector.tensor_tensor(
                        out=T2[:, 0:2, :], in0=T1[:, 0:2, :], in1=T1[:, 0:2, :], op=ALU.mult),
                        (("T1", 0), ("T1", 1)), (("T2", 0), ("T2", 1))))
                    ops.append(("dve", lambda: nc.vector.tensor_reduce(
                        out=SQ[:, 0:2], in_=T2[:, 0:2, :], axis=AX.X, op=ALU.add),
                        (("T2", 0), ("T2", 1)), (("SQ",),)))
                    ops.append(("act", lambda: nc.scalar.activation(
                        out=SQ[:, 4:6], in_=SQ[:, 0:2], func=AF.Ln, scale=1.0 / 128, bias=EPSC[:, 0:1]),
                        (("SQ",), ("EPSC",)), (("SQ2",),)))
                    ops.append(("act", lambda: nc.scalar.activation(
                        out=SQ[:, 4:6], in_=SQ[:, 4:6], func=AF.Exp, scale=-0.5),
                        (("SQ2",),), (("SQ2",),)))
                    for jq in range(2):
                        t = tts[jq]
                        ops.append(("dve", lambda jq=jq, t=t: nc.vector.tensor_tensor(
                            out=T2[:, jq, :], in0=YG[:, t, cols], in1=SUBG[:, j, :], op=ALU.mult),
                            (("YG", t), ("SUBG",)), (("T2", jq),)))
                        ops.append(("dve", lambda jq=jq, t=t: nc.vector.scalar_tensor_tensor(
                            out=YG[:, t, cols], in0=T1[:, jq, :], scalar=SQ[:, 4 + jq:5 + jq],
                            in1=T2[:, jq, :], op0=ALU.mult, op1=ALU.mult),
                            (("T1", jq), ("T2", jq), ("SQ2",)), (("YG", t),)))
                    return ops

                def pop_pending(k):
                    for _ in range(k):
                        if not pending:
                            return
                        eng, fn, reads, writes = pending.pop(0)
                        sc.op(eng, fn, reads=reads, writes=writes)

                npop = 2 if per_group >= 8 else 4
                for i0 in range(min(_LOOK, n)):
                    emit_s(i0)
                hook_at = n // 2
                toks = {0: emit_softmax(0)}
                for idx in range(n):
                    if idx + 1 < n:
                        toks[idx + 1] = emit_softmax(idx + 1)
                    pop_pending(npop)
                    if idx + _LOOK < n:
                        emit_s(idx + _LOOK)
                    for _ in range(_NFILL):
                        sc.op("pe", lambda: nc.tensor.matmul(
                            ST4[:, 1536:1536 + 128], lhsT=IDB[:, :], rhs=IDB[:, :], start=True, stop=True),
                            reads=(("IDB",),), writes=(("DUM",),), inc=False)
                    emit_pv(idx, None)
                    g, kt, first, last = steps[idx]
                    if last:
                        if dv != 64:
                            for half in range(2):
                                bank, bkey = oa_bank(g, half)
                                sc.op("dve", lambda: nc.vector.tensor_copy(
                                    out=OC[:, half, :, 0:129],
                                    in_=bank[:, :].rearrange("p (a c) -> p a c", a=2)[:, :, 0:129]),
                                    reads=(bkey,), writes=(("OC", half),))
                        pending.extend(evac_ops(g))
                    if idx == hook_at and mid_hook is not None:
                        mid_hook()
                pop_pending(len(pending))

            npairs = 8
            proj_tables(0)
            proj(0)
            for p in range(npairs):
                nxt = p + 1 < npairs
                kp = pair_info(p)["kind"]
                mid_tables = nxt and pair_info(p + 1)["kind"] == kp and kp in ("dif", "odd")

                def hook(p=p, mid_tables=mid_tables):
                    if mid_tables:
                        proj_tables(p + 1)
                    proj(p + 1)
                attend(p, hook if nxt else None)
                if nxt and not mid_tables:
                    proj_tables(p + 1)

            transpose_to_HT(False, l)
            for c in range(4):
                slot = c % 2
                sc.dma("pool", "wb%d" % slot,
                       [(WB[:, slot, :, :], w_out_v[:, :, c * 256:(c + 1) * 256], (), (("WB", slot),))])
                for tp in range(NT // 2):
                    bank, bkey = scratch()
                    for i in range(2):
                        t = tp * 2 + i
                        for kc in range(KC):
                            sc.op("pe", lambda i=i, t=t, kc=kc: nc.tensor.matmul(
                                bank[:, i * 256:(i + 1) * 256],
                                lhsT=HT[:, kc, t * 128:(t + 1) * 128], rhs=WB[:, slot, kc, :],
                                start=(kc == 0), stop=(kc == KC - 1)),
                                reads=HTK(t // 4) + (("WB", slot),), writes=(bkey,),
                                inc=(i == 1 and kc == KC - 1))
                    ts_ = tp % 2
                    sc.op("dve", lambda tp=tp, ts_=ts_: nc.vector.tensor_tensor(
                        out=TMPO[:, ts_, :, :], in0=bank[:, :].rearrange("p (a b) -> p a b", a=2),
                        in1=GREP[:, c * 256:(c + 1) * 256].unsqueeze(1).broadcast_to([128, 2, 256]),
                        op=ALU.mult),
                        reads=(bkey, ("GREP", 2 * c), ("GREP", 2 * c + 1)), writes=(("TMPO", ts_),))
                    xs = X[:, tp * 2:tp * 2 + 2, c * 256:(c + 1) * 256]
                    sc.op("dve", lambda tp=tp, ts_=ts_, xs=xs: nc.vector.tensor_tensor(
                        out=xs, in0=xs, in1=TMPO[:, ts_, :, :], op=ALU.add),
                        reads=(("TMPO", ts_), ("X", tp * 2), ("X", tp * 2 + 1)),
                        writes=(("X", tp * 2), ("X", tp * 2 + 1)))

        for l in range(_NLAYERS):
            layer(l)

        sc.barrier()
        sc.dma("sp", "small", [(GREP[:, :], fg_d[0].partition_broadcast(128), (), tuple(("GREP", fc) for fc in range(KC)))])
        rms_stats(DEPTH)
        ov_ = out_d.rearrange("(t p) d -> p t d", p=128)
        out_toks = []
        for t in range(NT):
            slot = t % 2
            ob = OUTB[:, slot * 1024:(slot + 1) * 1024]
            sc.op("dve", lambda t=t, ob=ob: nc.vector.scalar_tensor_tensor(
                out=ob, in0=X[:, t, :], scalar=RSTD[:, t:t + 1], in1=GREP[:, :],
                op0=ALU.mult, op1=ALU.mult),
                reads=(("X", t), ("RSTD",)) + tuple(("GREP", fc) for fc in range(KC)), writes=(("OUTB", slot),))
            out_toks.append(sc.dma("sp", "out%d" % slot,
                                   [(ov_[:, t, :], ob, (("OUTB", slot),), ())]))
        sc._wait("sp", ("out0", sc.cnt["out0"]))
        sc._wait("sp", ("out1", sc.cnt["out1"]))
    _CACHE['sched'] = sc
    return nc


_CACHE = {}


def kernel(x, c, ada_w, ada_b, norm_g, ab_w_in, ab_w_out, diff_lq1, diff_lk1, diff_lq2,
           diff_lk2, diff_subln_g, c_w_in, c_w_out, c_sink, final_g):
    f = lambda a: np.ascontiguousarray(np.asarray(a, dtype=np.float32))
    x = f(x)
    c = f(c)
    if "nc" not in _CACHE:
        _CACHE["nc"] = build_nc()
        _CACHE["tabs"] = _tables()
    nc = _CACHE["nc"]
    tdil, tdif, todd = _CACHE["tabs"]
    adab = np.ascontiguousarray(f(ada_b).reshape(DEPTH, 24, 128).transpose(2, 0, 1))
    ng = np.ascontiguousarray(f(norm_g).reshape(DEPTH, 8, 128).transpose(2, 0, 1))
    shared = {
        "ada_w": f(ada_w), "adab": adab, "ng": ng,
        "ab_w_in": f(ab_w_in), "ab_w_out": f(ab_w_out),
        "lq1": f(diff_lq1), "lk1": f(diff_lk1), "lq2": f(diff_lq2), "lk2": f(diff_lk2),
        "subg": f(diff_subln_g), "c_w_in": f(c_w_in), "c_w_out": f(c_w_out),
        "c_sink": f(c_sink), "final_g": f(final_g).reshape(1, D),
        "t_dil": tdil, "t_dif": tdif, "t_odd": todd,
        "idb": np.eye(128, dtype=np.float32).astype(ml_dtypes.bfloat16),
        "idf": np.eye(128, dtype=np.float32),
    }
    in_maps = []
    for b in range(8):
        m = dict(shared)
        m["x"] = x[b]
        m["ct"] = np.ascontiguousarray(c[b].reshape(KC, 128).T)
        in_maps.append(m)
    res = run_bass_kernel_spmd(nc, in_maps, core_ids=list(range(8)))
    return np.stack([np.asarray(r["out"], dtype=np.float32) for r in res.results], axis=0)
```
